# Optimizing a Trainium2 kernel written in Bass

```python
import math
import jax
import jax.numpy as jnp
from jax import lax

D_MODEL = 4096
BATCH = 4
SEQ = 4096
DEPTH = 1

CTX_LEN = 256
GRID_W = 64
EPS = 1e-6

POOL_W = D_MODEL // 2
POOL_GROUPS = 4
POOL_GC = POOL_W // POOL_GROUPS
POOL_WINDOWS = (2, 4, 8, 16)

SSD_INNER = D_MODEL // 2
SSD_HEAD_DIM = 64
SSD_HEADS = SSD_INNER // SSD_HEAD_DIM
SSD_GROUPS = 8
SSD_REP = SSD_HEADS // SSD_GROUPS
SSD_STATE = 128
SSD_CONV = 5
SSD_CHUNK = 128
SSD_GN = SSD_GROUPS * SSD_STATE
XBC_W = SSD_INNER + 2 * SSD_GN

OFF_Z = POOL_W
OFF_XBC = OFF_Z + SSD_INNER
OFF_DT = OFF_XBC + XBC_W
IN_W = OFF_DT + 2 * SSD_HEADS
MIX_W = POOL_W + SSD_INNER

PEER_HEADS = 8
PEER_NKEYS = 128
PEER_EXPERTS = PEER_NKEYS * PEER_NKEYS
PEER_KEY_DIM = 256
PEER_HALF = PEER_KEY_DIM // 2
PEER_TOPK = 16
PEER_BLOCK = 128

kernel_name = 'hybrid_pool_ssd_peer_prefix_dit'


def rmsnorm(x, g):
    xf = x.astype(jnp.float32)
    y = xf * lax.rsqrt(jnp.mean(xf * xf, axis=-1, keepdims=True) + EPS)
    return (y * g.astype(jnp.float32)).astype(x.dtype)


def window_mean(v, w, axis):
    n = v.shape[axis]
    vf = v.astype(jnp.float32)
    zero_shape = list(vf.shape)
    zero_shape[axis] = 1
    cs = jnp.concatenate([jnp.zeros(zero_shape, jnp.float32), jnp.cumsum(vf, axis=axis)], axis=axis)
    t = jnp.arange(n)
    lo = jnp.clip(t - w // 2, 0, n)
    hi = jnp.clip(t - w // 2 + w, 0, n)
    total = jnp.take(cs, hi, axis=axis) - jnp.take(cs, lo, axis=axis)
    cnt_shape = [1] * vf.ndim
    cnt_shape[axis] = n
    cnt = (hi - lo).astype(jnp.float32).reshape(cnt_shape)
    return (total / cnt).astype(v.dtype)


def pool_mixer(v, w_grp, scale, rows):
    bsz, n, _ = v.shape
    vg = v.reshape(bsz, n, POOL_GROUPS, POOL_GC)
    diffs = []
    for k, w in enumerate(POOL_WINDOWS):
        vk = vg[:, :, k]
        if rows is None:
            m = window_mean(vk, w, 1)
        else:
            grid = vk.reshape(bsz, rows, GRID_W, POOL_GC)
            m = window_mean(window_mean(grid, w, 1), w, 2).reshape(bsz, n, POOL_GC)
        diffs.append(m - vk)
    p = jnp.stack(diffs, axis=2)
    y = jnp.einsum('bngc,gcd->bngd', p, w_grp).reshape(bsz, n, POOL_W)
    return y * scale


def dwconv(u, w, b):
    pad = SSD_CONV // 2
    y = lax.conv_general_dilated(u, w[:, None, :], window_strides=(1,), padding=[(pad, pad)],
                                 dimension_numbers=('NWC', 'WIO', 'NWC'),
                                 feature_group_count=u.shape[-1])
    return y + b


def ssd_scan(xh, dt, a, bm, cm, h0, with_output):
    bsz, n = xh.shape[:2]
    nc = n // SSD_CHUNK
    q = SSD_CHUNK
    xh = xh.reshape(bsz, nc, q, SSD_GROUPS, SSD_REP, SSD_HEAD_DIM)
    dt = dt.reshape(bsz, nc, q, SSD_GROUPS, SSD_REP)
    bm = bm.reshape(bsz, nc, q, SSD_GROUPS, SSD_STATE)
    cm = cm.reshape(bsz, nc, q, SSD_GROUPS, SSD_STATE)
    a_cs = jnp.cumsum(dt * a.reshape(SSD_GROUPS, SSD_REP), axis=2)
    decay_end = jnp.exp(a_cs[:, :, -1:] - a_cs)
    states = jnp.einsum('bcjgn,bcjgr,bcjgrp->bcgrpn', bm, decay_end * dt, xh)
    chunk_decay = jnp.exp(a_cs[:, :, -1])

    def step(h, inp):
        dec, st = inp
        return h * dec[..., None, None] + st, h

    h_last, h_prev = lax.scan(step, h0, (jnp.moveaxis(chunk_decay, 1, 0), jnp.moveaxis(states, 1, 0)))
    if not with_output:
        return None, h_last
    h_prev = jnp.moveaxis(h_prev, 0, 1)
    causal = jnp.tril(jnp.ones((q, q), bool))[:, :, None, None]
    seg = a_cs[:, :, :, None] - a_cs[:, :, None, :]
    decay_ij = jnp.exp(jnp.where(causal, seg, -jnp.inf))
    cb = jnp.einsum('bcign,bcjgn->bcijg', cm, bm)
    m = cb[..., None] * decay_ij * dt[:, :, None]
    y_diag = jnp.einsum('bcijgr,bcjgrp->bcigrp', m, xh)
    y_off = jnp.einsum('bcign,bcigr,bcgrpn->bcigrp', cm, jnp.exp(a_cs), h_prev)
    return (y_diag + y_off).reshape(bsz, n, SSD_HEADS, SSD_HEAD_DIM), h_last


def ssd_mixer(xbc_raw, dt_raw, conv_w, conv_b, dt_bias, a_log, h0_fwd, h0_bwd, with_output):
    bsz, n, _ = xbc_raw.shape
    xbc = jax.nn.silu(dwconv(xbc_raw, conv_w, conv_b)).astype(jnp.float32)
    xh = xbc[..., :SSD_INNER].reshape(bsz, n, SSD_HEADS, SSD_HEAD_DIM)
    bm = xbc[..., SSD_INNER:SSD_INNER + SSD_GN].reshape(bsz, n, SSD_GROUPS, SSD_STATE)
    cm = xbc[..., SSD_INNER + SSD_GN:].reshape(bsz, n, SSD_GROUPS, SSD_STATE)
    dt = jax.nn.softplus(dt_raw.astype(jnp.float32).reshape(bsz, n, 2, SSD_HEADS) + dt_bias.astype(jnp.float32))
    a = -jnp.exp(a_log.astype(jnp.float32))
    flip = lambda t: jnp.flip(t, axis=1)
    y_f, h_f = ssd_scan(xh, dt[:, :, 0], a[0], bm, cm, h0_fwd, with_output)
    y_b, h_b = ssd_scan(flip(xh), flip(dt[:, :, 1]), a[1], flip(bm), flip(cm), h0_bwd, with_output)
    y = (y_f + flip(y_b)) if with_output else None
    return y, xh, h_f, h_b


def mixer_output(proj, y_ssd, xh, pool_w, pool_scale, d_skip, ssd_norm, w_out, rows):
    bsz, n, _ = proj.shape
    pooled = pool_mixer(proj[..., :POOL_W], pool_w, pool_scale, rows)
    y = (y_ssd + d_skip.astype(jnp.float32)[:, None] * xh).reshape(bsz, n, SSD_INNER)
    z = proj[..., OFF_Z:OFF_XBC].astype(jnp.float32)
    ssd = rmsnorm(y * jax.nn.silu(z), ssd_norm).astype(proj.dtype)
    return jnp.concatenate([pooled.astype(proj.dtype), ssd], axis=-1) @ w_out


def peer_ffn(h, wq, sub_keys, u_tab, v_tab):
    bsz, n, d = h.shape
    tokens = bsz * n
    qry = (h @ wq).reshape(bsz, n, PEER_HEADS, 2, PEER_HALF).astype(jnp.float32)
    s = jnp.einsum('bnhsk,hsek->bnhse', qry, sub_keys.astype(jnp.float32))
    s_top, i_top = lax.top_k(s, PEER_TOPK)
    cand = (s_top[..., 0, :, None] + s_top[..., 1, None, :]).reshape(bsz, n, PEER_HEADS, PEER_TOPK * PEER_TOPK)
    cand_idx = (i_top[..., 0, :, None] * PEER_NKEYS + i_top[..., 1, None, :]).reshape(bsz, n, PEER_HEADS, PEER_TOPK * PEER_TOPK)
    g_top, pos = lax.top_k(cand, PEER_TOPK)
    idx = jnp.take_along_axis(cand_idx, pos, axis=-1)
    gate = jax.nn.softmax(g_top, axis=-1).astype(h.dtype)
    nb = tokens // PEER_BLOCK
    hb = h.reshape(nb, PEER_BLOCK, d)
    ib = idx.reshape(nb, PEER_BLOCK, PEER_HEADS * PEER_TOPK)
    gb = gate.reshape(nb, PEER_BLOCK, PEER_HEADS * PEER_TOPK)

    def block(args):
        hx, ix, gx = args
        u = jnp.take(u_tab, ix, axis=0)
        act = jax.nn.gelu(jnp.einsum('td,tkd->tk', hx, u), approximate=False)
        v = jnp.take(v_tab, ix, axis=0)
        return jnp.einsum('tk,tkd->td', act * gx, v)

    return lax.map(block, (hb, ib, gb)).reshape(bsz, n, d)


def setup_inputs(seed: int = 0) -> dict:
    key = jax.random.key(seed)
    ks = jax.random.split(key, 26)
    f32 = jnp.float32
    nrm = lambda k, shape, sc: jax.random.normal(k, shape, f32) * sc
    dt0 = jnp.exp(jax.random.uniform(ks[13], (DEPTH, 2, SSD_HEADS), f32, math.log(1e-3), math.log(1e-1)))
    return {
        'x': nrm(ks[0], (BATCH, SEQ, D_MODEL), 1.0),
        'c': nrm(ks[1], (BATCH, D_MODEL), 1.0),
        'ctx': nrm(ks[2], (BATCH, CTX_LEN, D_MODEL), 1.0),
        'c_ctx': nrm(ks[3], (D_MODEL,), 1.0),
        'w_mod': nrm(ks[4], (DEPTH, D_MODEL, 6 * D_MODEL), 0.5 * D_MODEL ** -0.5),
        'b_mod': nrm(ks[5], (DEPTH, 6 * D_MODEL), 0.02),
        'norm1': 1.0 + nrm(ks[6], (DEPTH, D_MODEL), 0.05),
        'norm2': 1.0 + nrm(ks[7], (DEPTH, D_MODEL), 0.05),
        'w_in': nrm(ks[8], (DEPTH, D_MODEL, IN_W), D_MODEL ** -0.5),
        'pool_w': nrm(ks[9], (DEPTH, POOL_GROUPS, POOL_GC, POOL_GC), POOL_GC ** -0.5),
        'pool_scale': 1.0 + nrm(ks[10], (DEPTH, POOL_W), 0.1),
        'conv_w': nrm(ks[11], (DEPTH, SSD_CONV, XBC_W), SSD_CONV ** -0.5),
        'conv_b': nrm(ks[12], (DEPTH, XBC_W), 0.02),
        'dt_bias': dt0 + jnp.log(-jnp.expm1(-dt0)),
        'a_log': jnp.log(jax.random.uniform(ks[14], (DEPTH, 2, SSD_HEADS), f32, 1.0, 16.0)),
        'd_skip': 1.0 + nrm(ks[15], (DEPTH, SSD_HEADS), 0.1),
        'ssd_norm': 1.0 + nrm(ks[16], (DEPTH, SSD_INNER), 0.05),
        'w_out': nrm(ks[17], (DEPTH, MIX_W, D_MODEL), MIX_W ** -0.5),
        'peer_wq': nrm(ks[18], (DEPTH, D_MODEL, PEER_HEADS * PEER_KEY_DIM), D_MODEL ** -0.5),
        'peer_keys': nrm(ks[19], (DEPTH, PEER_HEADS, 2, PEER_NKEYS, PEER_HALF), PEER_HALF ** -0.5),
        'peer_u': nrm(ks[20], (DEPTH, PEER_EXPERTS, D_MODEL), D_MODEL ** -0.5),
        'peer_v': nrm(ks[21], (DEPTH, PEER_EXPERTS, D_MODEL), (PEER_HEADS * PEER_TOPK) ** -0.5),
        'final_norm': 1.0 + nrm(ks[22], (D_MODEL,), 0.05),
    }


def reference(x, c, ctx, c_ctx, w_mod, b_mod, norm1, norm2, w_in, pool_w, pool_scale,
              conv_w, conv_b, dt_bias, a_log, d_skip, ssd_norm, w_out,
              peer_wq, peer_keys, peer_u, peer_v, final_norm):
    bsz, n_lat, _ = x.shape
    rows = n_lat // GRID_W
    h0 = jnp.zeros((bsz, SSD_GROUPS, SSD_REP, SSD_HEAD_DIM, SSD_STATE), jnp.float32)
    for l in range(DEPTH):
        last = l == DEPTH - 1
        mod = jax.nn.silu(c) @ w_mod[l] + b_mod[l]
        sh1, sc1, g1, sh2, sc2, g2 = jnp.split(mod[:, None, :], 6, axis=-1)
        cmod = jax.nn.silu(c_ctx) @ w_mod[l] + b_mod[l]
        csh1, csc1, cg1, csh2, csc2, cg2 = jnp.split(cmod, 6)
        hc = rmsnorm(ctx, norm1[l]) * (1.0 + csc1) + csh1
        pc = hc @ (w_in[l][:, OFF_XBC:] if last else w_in[l])
        pc_ssd = pc if last else pc[..., OFF_XBC:]
        yc, xc, hf, hb = ssd_mixer(pc_ssd[..., :XBC_W], pc_ssd[..., XBC_W:], conv_w[l], conv_b[l],
                                   dt_bias[l], a_log[l], h0, h0, not last)
        h = rmsnorm(x, norm1[l]) * (1.0 + sc1) + sh1
        p = h @ w_in[l]
        y, xs, _, _ = ssd_mixer(p[..., OFF_XBC:OFF_DT], p[..., OFF_DT:], conv_w[l], conv_b[l],
                                dt_bias[l], a_log[l], hf, hb, True)
        x = x + g1 * mixer_output(p, y, xs, pool_w[l], pool_scale[l], d_skip[l], ssd_norm[l], w_out[l], rows)
        x = x + g2 * peer_ffn(rmsnorm(x, norm2[l]) * (1.0 + sc2) + sh2,
                              peer_wq[l], peer_keys[l], peer_u[l], peer_v[l])
        if not last:
            ctx = ctx + cg1 * mixer_output(pc, yc, xc, pool_w[l], pool_scale[l], d_skip[l], ssd_norm[l], w_out[l], None)
            ctx = ctx + cg2 * peer_ffn(rmsnorm(ctx, norm2[l]) * (1.0 + csc2) + csh2,
                                       peer_wq[l], peer_keys[l], peer_u[l], peer_v[l])
    return rmsnorm(x, final_norm)
```

```python
import contextlib
import numpy as np
import concourse.bass as bass
import concourse.mybir as mybir
from concourse.bass_utils import run_bass_kernel_spmd

F32 = mybir.dt.float32
BF16 = mybir.dt.bfloat16
AF = mybir.ActivationFunctionType
ALU = mybir.AluOpType
AX = mybir.AxisListType

D = 4096
NT = 4096
OWN = 2048
EPS = 1e-6
IN_W = 8256
NEG = -1.0e30


class Buf:
    __slots__ = ("name", "last_w", "readers")

    def __init__(self, name):
        self.name = name
        self.last_w = None
        self.readers = []


class _Grp:
    __slots__ = ("sem_key", "final", "ops")


class _Op:
    __slots__ = ("eng", "fn", "deps", "is_dma", "grp", "val", "needs_sig")


class Prog:
    ENGS = ("pe", "act", "dve", "pool", "sp")

    def __init__(self, nc):
        self.nc = nc
        self.ops = []
        self.dma_keys = {}
        self.final_groups = []
        self.bar_deps = set()
        self.bar_id = 0
        self.eng_bar = {e: 0 for e in self.ENGS}
        self.last_op = {e: None for e in self.ENGS}
        self.dma_since_bar = []

    def barrier(self):
        deps = set()
        for e in self.ENGS:
            if self.last_op[e] is not None:
                deps.add(self.last_op[e])
        for g in self.dma_since_bar:
            deps.add(g.ops[0])
        self.dma_since_bar = []
        self.bar_deps = deps
        self.bar_id += 1

    def _deps(self, eng, reads, writes):
        deps = set()
        for b in reads:
            if b.last_w is not None:
                deps.add(b.last_w)
        for b in writes:
            if b.last_w is not None:
                deps.add(b.last_w)
            deps.update(b.readers)
        if self.eng_bar[eng] != self.bar_id:
            deps.update(self.bar_deps)
            self.eng_bar[eng] = self.bar_id
        return deps

    def op(self, eng, fn, reads=(), writes=()):
        o = _Op()
        o.eng = eng
        o.fn = fn
        o.is_dma = False
        o.grp = None
        o.needs_sig = False
        o.deps = self._deps(eng, reads, writes)
        for b in reads:
            b.readers.append(o)
        for b in writes:
            b.last_w = o
            b.readers = []
        self.ops.append(o)
        self.last_op[eng] = o
        return o

    def dma(self, items, reads, writes, key, final=False, nobar=False, family=False):
        st = self.dma_keys.setdefault(key, {"count": 0, "last": None})
        g = _Grp()
        g.sem_key = key
        g.ops = []
        deps = set()
        for it in items:
            deps |= self._deps(it[0], reads, writes)
        if family:
            st["family"] = True
        elif st["last"] is not None:
            deps.update(st["last"].ops)
        for it in items:
            q, out, in_ = it[0], it[1], it[2]
            kw = it[3] if len(it) > 3 else {}
            o = _Op()
            o.eng = q
            o.is_dma = True
            o.grp = g
            o.needs_sig = True
            o.deps = deps
            o.fn = (lambda e, out=out, in_=in_, kw=kw: e.dma_start(out=out, in_=in_, **kw))
            g.ops.append(o)
            self.ops.append(o)
            st["count"] += 1
        g.final = 16 * st["count"]
        st["last"] = g
        for b in reads:
            b.readers.append(g.ops[0])
        for b in writes:
            b.last_w = g.ops[0]
            b.readers = []
        if not nobar:
            self.dma_since_bar.append(g)
        if final:
            self.final_groups.append(g)
        return g

    def emit(self, stack):
        nc = self.nc
        for o in self.ops:
            for d in o.deps:
                if not d.is_dma:
                    if d.eng == "pe" and o.eng == "pe" and not o.is_dma:
                        continue
                    d.needs_sig = True
        esem = {e: stack.enter_context(nc.semaphore("s_" + e)) for e in self.ENGS}
        dsem = {}
        for i, k in enumerate(self.dma_keys):
            dsem[k] = stack.enter_context(nc.semaphore("d%d" % i))
        cnt = {e: 0 for e in self.ENGS}
        for o in self.ops:
            if not o.is_dma and o.needs_sig:
                cnt[o.eng] += 1
                o.val = cnt[o.eng]
        per = {e: [o for o in self.ops if o.eng == e] for e in self.ENGS}
        finals = self.final_groups

        def run(eng_name, e):
            waited = {}
            for o in per[eng_name]:
                need = {}
                for d in o.deps:
                    if d.is_dma:
                        s, v = dsem[d.grp.sem_key], d.grp.final
                        if self.dma_keys[d.grp.sem_key].get("family"):
                            v = 16 * self.dma_keys[d.grp.sem_key]["count"]
                    else:
                        if d.eng == "pe" and eng_name == "pe" and not o.is_dma:
                            continue
                        s, v = esem[d.eng], d.val
                    if need.get(s, 0) < v:
                        need[s] = v
                for s, v in need.items():
                    if waited.get(s, 0) < v:
                        e.wait_ge(s, v)
                        waited[s] = v
                ins = o.fn(e)
                if o.is_dma:
                    ins.then_inc(dsem[o.grp.sem_key], 16)
                elif o.needs_sig:
                    ins.then_inc(esem[eng_name], 1)
            if eng_name == "sp":
                for g in finals:
                    v = g.final
                    if self.dma_keys[g.sem_key].get("family"):
                        v = 16 * self.dma_keys[g.sem_key]["count"]
                    e.wait_ge(dsem[g.sem_key], v)

        block = stack.enter_context(nc.Block())
        block.sync(lambda e: run("sp", e))
        block.scalar(lambda e: run("act", e))
        block.vector(lambda e: run("dve", e))
        block.gpsimd(lambda e: run("pool", e))
        block.tensor(lambda e: run("pe", e))


POOL_WINDOWS = (2, 4, 8, 16)
POOL_ND = (1, 1, 2, 4)


def _pool_consts(half):
    pm = np.zeros((128, 4, 9, 128), np.float32)
    invc = np.zeros((128, 16, 4), np.float32)
    pin = np.arange(128)
    for wi, w in enumerate(POOL_WINDOWS):
        lo, hi = (-(w // 2), w // 2 - 1) if half == 0 else (-(w // 2) + 1, w // 2)
        for d in range(-4, 5):
            dr = 2 * d + (pin[:, None] // 64) - (pin[None, :] // 64)
            dc = (pin[:, None] % 64) - (pin[None, :] % 64)
            pm[:, wi, d + 4, :] = ((dr >= lo) & (dr <= hi) & (dc >= lo) & (dc <= hi)).astype(np.float32)
        idx = np.arange(64)
        cnt = np.array([np.sum((idx >= r + lo) & (idx <= r + hi)) for r in range(64)], np.float32)
        t = np.arange(2048)
        ic = 1.0 / (cnt[t // 64] * cnt[t % 64])
        invc[:, :, wi] = ic.reshape(16, 128).T
    return pm, invc


def _tri_consts():
    k = np.arange(128)[:, None]
    i = np.arange(128)[None, :]
    c = np.zeros((128, 6, 128), np.float32)
    c[:, 0] = (k <= i)
    c[:, 1] = (k >= i)
    c[:, 2] = (k > i)
    c[:, 3] = (k < i)
    c[:, 4] = 1.0
    c[:, 5] = (k == i)
    return c


def prep_core(inp, b, half):
    f = np.float32
    flip = half == 1
    xl = inp["x"][b][::-1] if flip else inp["x"][b]
    cl = inp["ctx"][b][::-1] if flip else inp["ctx"][b]
    w_in = inp["w_in"][0]
    dtb = inp["dt_bias"][0]
    alog = inp["a_log"][0]
    cw = inp["conv_w"][0]
    if flip:
        w_in = np.concatenate([w_in[:, :8192], w_in[:, 8224:8256], w_in[:, 8192:8224]], axis=1)
        dtb = dtb[::-1]
        alog = alog[::-1]
        cw = cw[::-1]
    pm, invc = _pool_consts(half)
    d = {
        "xl": np.ascontiguousarray(xl, f),
        "ctxl": np.ascontiguousarray(cl, f),
        "cvec": np.ascontiguousarray(np.stack([inp["c"][b], inp["c_ctx"]]), f),
        "w_mod": inp["w_mod"][0],
        "b_mod": inp["b_mod"],
        "nrm": np.ascontiguousarray(np.stack([inp["norm1"][0], inp["norm2"][0], inp["final_norm"]]), f),
        "w_in": np.ascontiguousarray(w_in, f),
        "pool_w": inp["pool_w"][0],
        "pool_scale": inp["pool_scale"],
        "conv_w": np.ascontiguousarray(cw, f),
        "conv_b": inp["conv_b"],
        "dt_bias": np.ascontiguousarray(dtb.reshape(1, 64), f),
        "a_log": np.ascontiguousarray(alog.reshape(1, 64), f),
        "d_skip": inp["d_skip"],
        "ssd_norm": inp["ssd_norm"],
        "w_out": inp["w_out"][0],
        "peer_wq": inp["peer_wq"][0],
        "peer_keys": np.ascontiguousarray(inp["peer_keys"][0].reshape(16, 128, 128), f),
        "peer_u": inp["peer_u"][0],
        "peer_v": inp["peer_v"][0],
        "tri": _tri_consts(),
        "pm": pm,
        "invc": invc,
    }
    return d


INPUT_SHAPES = {
    "xl": [NT, D], "ctxl": [256, D], "cvec": [2, D], "w_mod": [D, 6 * D], "b_mod": [1, 6 * D],
    "nrm": [3, D], "w_in": [D, IN_W], "pool_w": [4, 512, 512], "pool_scale": [1, 2048],
    "conv_w": [5, D], "conv_b": [1, D], "dt_bias": [1, 64], "a_log": [1, 64], "d_skip": [1, 32],
    "ssd_norm": [1, 2048], "w_out": [D, D], "peer_wq": [D, 2048], "peer_keys": [16, 128, 128],
    "peer_u": [16384, D], "peer_v": [16384, D], "tri": [128, 6, 128], "pm": [128, 4, 9, 128],
    "invc": [128, 16, 4],
}


class Ctx:
    pass


def build_program(stop=99, dbg=()):
    nc = bass.Bass("TRN2", target_bir_lowering=False)
    I = {k: nc.dram_tensor(k, shp, F32, kind="ExternalInput").ap() for k, shp in INPUT_SHAPES.items()}
    out = nc.dram_tensor("out", [OWN, D], F32, kind="ExternalOutput").ap()

    def scratch(name, shape, dt):
        kind = "ExternalOutput" if name in dbg else "Internal"
        return nc.dram_tensor(name, shape, dt, kind=kind).ap()

    S = {}
    S["modrow"] = scratch("modrow", [2, 6 * D], F32)
    S["w_inB"] = scratch("w_inB", [16, 128, 32 * 512], BF16)
    S["w_dtB"] = scratch("w_dtB", [128, 32 * 64], BF16)
    S["w_outB"] = scratch("w_outB", [2, 4, 128, 8 * 2048], BF16)
    S["wqB"] = scratch("wqB", [8, 128, 32 * 256], BF16)
    S["poolwB"] = scratch("poolwB", [4, 512, 512], BF16)
    S["VB"] = scratch("VB", [2, 16, 128, 8 * 2048], BF16)
    S["UT"] = scratch("UT", [32, 128, 32 * 512], BF16)
    S["XH"] = scratch("XH", [5120, 2048], BF16)
    S["BMK"] = scratch("BMK", [5120, 1024], BF16)
    S["BMT"] = scratch("BMT", [8, 128, 5120], BF16)
    S["CMT"] = scratch("CMT", [8, 128, 5120], BF16)
    S["DT"] = scratch("DT", [NT + 256, 64], F32)
    S["SZ"] = scratch("SZ", [OWN, 2048], BF16)
    S["VV"] = scratch("VV", [2560, 2048], BF16)
    S["HFS"] = scratch("HFS", [16, 128, 2048], BF16)
    S["MIXT"] = scratch("MIXT", [32, 128, OWN], BF16)
    S["X1"] = scratch("X1", [OWN, D], F32)
    S["H2T"] = scratch("H2T", [32, 128, OWN], BF16)
    S["SS"] = scratch("SS", [OWN, 2048], F32)
    S["TN"] = scratch("TN", [OWN, 16], F32)
    S["PO"] = scratch("PO", [OWN, D], F32)
    S["HCTX"] = scratch("HCTX", [2, 128, 2048], F32)

    bufs = {}

    def BB(*key):
        b = bufs.get(key)
        if b is None:
            b = bufs[key] = Buf(str(key))
        return b

    with contextlib.ExitStack() as st:
        P = Prog(nc)
        K = Ctx()
        K.nc, K.P, K.I, K.S, K.BB, K.out = nc, P, I, S, BB, out

        def psb(name, shape, dt):
            return st.enter_context(nc.sbuf_tensor("g_" + name, shape, dt))

        K.tri = psb("tri", [128, 6, 128], F32)
        K.ident = K.tri[:, 5, :]
        K.colv = psb("colv", [128, 8, 32], F32)
        K.gam = psb("gam", [128, 3, 32], F32)
        P.dma([("sp", K.tri[:], I["tri"])], [], [BB("tri")], BB("tri"))

        def stage(dst, src, rows, step, name):
            for r0 in range(0, rows, step):
                P.dma([("pool", dst[r0:r0 + step], src[r0:r0 + step])], [], [BB(name, r0)], BB(name, "k"))
        for cb in range(16):
            P.dma([("pool", S["w_inB"][cb].rearrange("p (kc n) -> p kc n", n=512),
                    I["w_in"][:, cb * 512:(cb + 1) * 512].rearrange("(kc p) n -> p kc n", p=128))],
                  [], [BB("w_inB", cb)], BB("w_inB", "k"), family=True)
        P.dma([("pool", S["w_dtB"].rearrange("p (kc n) -> p kc n", n=64),
                I["w_in"][:, 8192:8256].rearrange("(kc p) n -> p kc n", p=128))], [], [BB("w_dtB")], BB("w_dtB", "k"))
        stage(S["poolwB"], I["pool_w"], 4, 4, "poolwB")

        def stage_late():
            for dmh in range(2):
                for fcg in range(4):
                    P.dma([("pool", S["w_outB"][dmh, fcg].rearrange("p (a n) -> p a n", n=2048),
                            I["w_out"][fcg * 1024:(fcg + 1) * 1024, dmh * 2048:(dmh + 1) * 2048].rearrange("(a p) n -> p a n", p=128))],
                          [], [BB("w_outB", dmh, fcg)], BB("w_outB", "k"), nobar=True, family=True)
            for cbk in range(8):
                P.dma([("pool", S["wqB"][cbk].rearrange("p (kc n) -> p kc n", n=256),
                        I["peer_wq"][:, cbk * 256:(cbk + 1) * 256].rearrange("(kc p) n -> p kc n", p=128))],
                      [], [BB("wqB", cbk)], BB("wqB", "k"), nobar=True, family=True)
        def stage_late2():
            for dmh in range(2):
                for ecg in range(16):
                    P.dma([("pool", S["VB"][dmh, ecg].rearrange("p (a n) -> p a n", n=2048),
                            I["peer_v"][ecg * 1024:(ecg + 1) * 1024, dmh * 2048:(dmh + 1) * 2048].rearrange("(a p) n -> p a n", p=128))],
                          [], [BB("VB", dmh, ecg)], BB("VB", "k"), nobar=True, family=True)
        K.stage_late = stage_late
        K.stage_late2 = stage_late2

        phase_mod(K)
        P.barrier()
        if stop >= 1:
            phase_inproj(K)
            P.barrier()
        if stop >= 2:
            phase_ssd(K)
            P.barrier()
        if stop >= 3:
            phase_pool(K)
            P.barrier()
        if stop >= 4:
            phase_wout(K)
            P.barrier()
        if stop >= 5:
            phase_peer_prep(K)
            P.barrier()
        if stop >= 6:
            phase_peer_dense(K)
            P.barrier()
        if stop >= 7:
            phase_final(K)
        else:
            with contextlib.ExitStack() as ph:
                t = ph.enter_context(nc.sbuf_tensor("g_dummy_o", [128, 64], F32))
                P.op("dve", lambda e: e.memset(t[:], 0.0), [], [BB("dummy_o")])
                P.dma([("sp", out[0:128, 0:64], t[:])], [BB("dummy_o")], [BB("out", "d")], BB("dummy_o"), final=True)
        for g in list(P.dma_since_bar):
            if g not in P.final_groups:
                P.final_groups.append(g)
        P.emit(st)
    return nc


def split_dma(P, out, in_, reads, writes, key, n=2, axis=1, queues=("sp", "act"), kw=None):
    size = out.shape[axis]
    step = size // n
    items = []
    for i in range(n):
        sl = [slice(None)] * len(out.shape)
        sl[axis] = slice(i * step, (i + 1) * step)
        sl = tuple(sl)
        it = (queues[i % len(queues)], out[sl], in_[sl])
        if kw:
            it = it + (kw,)
        items.append(it)
    return P.dma(items, reads, writes, key)


def phase_mod(K):
    nc, P, I, S, BB = K.nc, K.P, K.I, K.S, K.BB
    slow = {"allow_slow_non_contiguous": True}
    with contextlib.ExitStack() as ph:
        sb = lambda n, s, dt: ph.enter_context(nc.sbuf_tensor("m_" + n, s, dt))
        cT = sb("cT", [128, 2, 32], F32)
        scT = sb("scT", [128, 32, 2], F32)
        wm = [sb("wm%d" % i, [128, 32, 512], F32) for i in range(2)]
        bm2 = [sb("bm2_%d" % i, [2, 512], F32) for i in range(2)]
        res = [sb("res%d" % i, [2, 512], F32) for i in range(2)]
        pm_ = [ph.enter_context(nc.psum_tensor("m_pm%d" % i, [128, 512], F32)) for i in range(2)]
        for r in range(2):
            P.dma([("sp", cT[:, r, :], I["cvec"][r, :].rearrange("(kc p) -> p kc", p=128), slow)],
                  [], [BB("cT", r)], BB("cT", r))
            P.op("act", lambda e, r=r: e.activation(scT[:, :, r], cT[:, r, :], AF.Silu), [BB("cT", r)], [BB("scT")])
        for cb in range(48):
            i = cb % 2
            cols = slice(cb * 512, (cb + 1) * 512)
            split_dma(P, wm[i][:], I["w_mod"][:, cols].rearrange("(kc p) n -> p kc n", p=128),
                      [], [BB("wm", i)], BB("wm", i), n=4, axis=1)
            P.dma([("sp", bm2[i][:], I["b_mod"][0:1, cols].partition_broadcast(2))], [], [BB("bm2", i)], BB("bm2", i))
            for kc in range(32):
                P.op("pe", lambda e, i=i, kc=kc: e.matmul(pm_[i][0:2, :], scT[:, kc, :], wm[i][:, kc, :],
                                                           start=(kc == 0), stop=(kc == 31)),
                     [BB("scT"), BB("wm", i)], [BB("pm", i)])
            P.op("dve", lambda e, i=i: e.tensor_tensor(res[i][:], pm_[i][0:2, :], bm2[i][:], ALU.add),
                 [BB("pm", i), BB("bm2", i)], [BB("res", i)])
            P.dma([("sp", S["modrow"][:, cols], res[i][:])], [BB("res", i)], [BB("modrow")], BB("res", i))
        srcs = [S["modrow"][0, 0:D], S["modrow"][0, D:2 * D], S["modrow"][0, 3 * D:4 * D], S["modrow"][0, 4 * D:5 * D],
                S["modrow"][1, 0:D], S["modrow"][1, D:2 * D], I["nrm"][0, :], I["nrm"][1, :]]
        for v, src in enumerate(srcs):
            P.dma([("sp" if v % 2 == 0 else "act", K.colv[:, v, :], src.rearrange("(kc p) -> p kc", p=128), slow)],
                  [BB("modrow")], [BB("colv", v)], BB("colv", v))
        for gi, (sci, ni) in enumerate([(1, 6), (3, 7), (5, 6)]):
            P.op("dve", lambda e, gi=gi, sci=sci, ni=ni: e.scalar_tensor_tensor(
                K.gam[:, gi, :], K.colv[:, sci, :], 1.0, K.colv[:, ni, :], ALU.add, ALU.mult),
                [BB("colv", sci), BB("colv", ni)], [BB("gam")])


def phase_inproj(K):
    nc, P, I, S, BB = K.nc, K.P, K.I, K.S, K.BB
    K.stage_late()
    slow = {"allow_slow_non_contiguous": True}
    with contextlib.ExitStack() as ph:
        sb = lambda n, s, dt: ph.enter_context(nc.sbuf_tensor("a_" + n, s, dt))
        xt = sb("xt", [128, D], F32)
        xs = sb("xs", [128, D], F32)
        ss = sb("ss", [128, 4], F32)
        hT = sb("hT", [128, 32, 512], BF16)
        wt = [sb("wt%d" % i, [128, 32, 512], BF16) for i in range(2)]
        wdt = sb("wdt", [128, 32, 64], BF16)
        pcv = [sb("pcv%d" % i, [128, 516], F32) for i in range(2)]
        a32 = [sb("a32_%d" % i, [128, 512], F32) for i in range(2)]
        xc32 = [sb("xc32_%d" % i, [128, 512], F32) for i in range(2)]
        xcb = [sb("xcb%d" % i, [128, 512], BF16) for i in range(2)]
        carry = sb("carry", [128, 32, 4], F32)
        cw = sb("cw", [128, 32, 5], F32)
        cbs = sb("cbs", [128, 32], F32)
        xh_tok = sb("xh_tok", [128, 4, 3072], BF16)
        vz_tok = sb("vz_tok", [128, 4, 2048], BF16)
        dtb = sb("dtb", [128, 64], F32)
        d1 = sb("d1", [128, 64], F32)
        d2 = sb("d2", [128, 64], F32)
        tp = [ph.enter_context(nc.psum_tensor("a_tp%d" % i, [128, 512], F32)) for i in range(2)]
        acc = [ph.enter_context(nc.psum_tensor("a_acc%d" % i, [128, 512], F32)) for i in range(4)]
        ptk = [ph.enter_context(nc.psum_tensor("a_ptk%d" % i, [128, 512], F32)) for i in range(2)]
        print("inproj sbuf remaining", nc.sbuf_bytes_remaining)

        P.dma([("sp", cw[:, :, k], I["conv_w"][k, :].rearrange("(cc p) -> p cc", p=128), slow) for k in range(5)],
              [], [BB("cw")], BB("cw"))
        P.dma([("act", cbs[:], I["conv_b"][0, :].rearrange("(cc p) -> p cc", p=128), slow)], [], [BB("cbs")], BB("cbs"))
        P.dma([("sp", dtb[:], I["dt_bias"][0:1, :].partition_broadcast(128))], [], [BB("dtb")], BB("dtb"))
        P.dma([("sp", wdt[:], S["w_dtB"].rearrange("p (kc n) -> p kc n", n=64))], [BB("w_dtB")], [BB("wdt")], BB("wdt"))
        cnt = {"wt": 0, "acc": 0, "tp": 0, "ptk": 0, "cc": 0}

        def load_w(col0):
            i = cnt["wt"] % 2
            cnt["wt"] += 1
            cb = col0 // 512
            P.dma([("sp", wt[i][:], S["w_inB"][cb].rearrange("p (kc n) -> p kc n", n=512))],
                  [BB("w_inB", cb)], [BB("wt", i)], BB("wt", i))
            return i

        def run_seq(src, ntok, BT, gi, bi, row0, dtrow0, flags):
            nq = BT // 128
            nblk = ntok // BT
            P.op("dve", lambda e: e.memset(carry[:], 0.0), [], [BB("carry")])
            for j in range(nblk + 1):
                flush = j == nblk
                fl = flags(j) if not flush else flags(nblk - 1)
                if not flush:
                    for tq in range(nq):
                        r = j * BT + tq * 128
                        P.dma([("sp", xt[:], src[r:r + 128, :])], [], [BB("xt")], BB("xt"))
                        P.op("dve", lambda e: e.memset(ss[:], 0.0), [], [BB("ss")])
                        P.op("act", lambda e: e.activation(xs[:], xt[:], AF.Square, accum_out=ss[:, 0:1]),
                             [BB("xt"), BB("ss")], [BB("xs"), BB("ss")])
                        P.op("dve", lambda e: e.tensor_scalar(ss[:, 1:2], ss[:, 0:1], 1.0 / D, EPS, ALU.mult, ALU.add),
                             [BB("ss")], [BB("ss")])
                        P.op("act", lambda e: e.activation(ss[:, 2:3], ss[:, 1:2], AF.Sqrt), [BB("ss")], [BB("ss")])
                        P.op("dve", lambda e: e.reciprocal(ss[:, 3:4], ss[:, 2:3]), [BB("ss")], [BB("ss")])
                        P.op("dve", lambda e: e.tensor_scalar(xs[:], xt[:], ss[:, 3:4], None, ALU.mult),
                             [BB("xt"), BB("ss")], [BB("xs")])
                        for kc in range(32):
                            if kc % 4 == 0:
                                ti = cnt["tp"] % 2
                                cnt["tp"] += 1
                            P.op("pe", lambda e, ti=ti, kc=kc: e.transpose(
                                tp[ti][:, (kc % 4) * 128:(kc % 4 + 1) * 128], xs[:, kc * 128:(kc + 1) * 128], K.ident),
                                [BB("xs"), BB("tri")], [BB("tp", ti)])
                            if kc % 4 == 3:
                                for k2 in range(kc - 3, kc + 1):
                                    P.op("act", lambda e, ti=ti, k2=k2, tq=tq: e.activation(
                                        hT[:, k2, tq * 128:(tq + 1) * 128], tp[ti][:, (k2 % 4) * 128:(k2 % 4 + 1) * 128],
                                        AF.Identity, bias=K.colv[:, bi, k2:k2 + 1], scale=K.gam[:, gi, k2:k2 + 1]),
                                        [BB("tp", ti), BB("gam"), BB("colv", bi)], [BB("hT", tq)])
                hT_bufs = [BB("hT", tq) for tq in range(nq)]
                ncc = 32 if fl["cm"] else 24
                for cbk in range(ncc // 4):
                    if not flush:
                        wi = load_w(4096 + cbk * 512)
                    for sub in range(4):
                        cc = cbk * 4 + sub
                        ci = cnt["cc"] % 2
                        cnt["cc"] += 1
                        if not flush:
                            ai = cnt["acc"] % 4
                            cnt["acc"] += 1
                            for kc in range(32):
                                P.op("pe", lambda e, ai=ai, wi=wi, kc=kc, sub=sub: e.matmul(
                                    acc[ai][:, 0:BT], wt[wi][:, kc, sub * 128:(sub + 1) * 128], hT[:, kc, 0:BT],
                                    start=(kc == 0), stop=(kc == 31)),
                                    [BB("wt", wi)] + hT_bufs, [BB("acc", ai)])
                            P.op("act", lambda e, ai=ai, ci=ci: e.copy(pcv[ci][:, 4:4 + BT], acc[ai][:, 0:BT]),
                                 [BB("acc", ai)], [BB("pcv", ci)])
                        else:
                            P.op("pool", lambda e, ci=ci: e.memset(pcv[ci][:, 4:4 + BT], 0.0), [], [BB("pcv", ci)])
                        P.op("pool", lambda e, ci=ci, cc=cc: e.tensor_copy(pcv[ci][:, 0:4], carry[:, cc, :]),
                             [BB("carry")], [BB("pcv", ci)])
                        P.op("pool", lambda e, ci=ci, cc=cc: e.tensor_copy(carry[:, cc, :], pcv[ci][:, BT:BT + 4]),
                             [BB("pcv", ci)], [BB("carry")])
                        P.op("dve", lambda e, ci=ci, cc=cc: e.tensor_scalar(
                            a32[ci][:, 0:BT], pcv[ci][:, 0:BT], cw[:, cc, 0:1], None, ALU.mult),
                            [BB("pcv", ci), BB("cw")], [BB("a32", ci)])
                        for k in range(1, 5):
                            P.op("dve", lambda e, ci=ci, cc=cc, k=k: e.scalar_tensor_tensor(
                                a32[ci][:, 0:BT], pcv[ci][:, k:k + BT], cw[:, cc, k:k + 1], a32[ci][:, 0:BT],
                                ALU.mult, ALU.add), [BB("pcv", ci), BB("cw"), BB("a32", ci)], [BB("a32", ci)])
                        P.op("act", lambda e, ci=ci, cc=cc: e.activation(
                            xc32[ci][:, 0:BT], a32[ci][:, 0:BT], AF.Silu, bias=cbs[:, cc:cc + 1]),
                            [BB("a32", ci), BB("cbs")], [BB("xc32", ci)])
                        if cc < 24:
                            pi = cnt["ptk"] % 2
                            cnt["ptk"] += 1
                            for tq in range(nq):
                                P.op("pe", lambda e, pi=pi, ci=ci, tq=tq: e.transpose(
                                    ptk[pi][:, tq * 128:(tq + 1) * 128], xc32[ci][:, tq * 128:(tq + 1) * 128], K.ident),
                                    [BB("xc32", ci), BB("tri")], [BB("ptk", pi)])
                            P.op("dve", lambda e, pi=pi, cc=cc: e.tensor_copy(
                                xh_tok[:, 0:nq, cc * 128:(cc + 1) * 128],
                                ptk[pi][:, 0:BT].rearrange("p (q c) -> p q c", c=128)),
                                [BB("ptk", pi)], [BB("xh_tok")])
                        if cc >= 16:
                            P.op("act", lambda e, ci=ci: e.copy(xcb[ci][:, 0:BT], xc32[ci][:, 0:BT]),
                                 [BB("xc32", ci)], [BB("xcb", ci)])
                            dst = S["BMT"][cc - 16] if cc < 24 else S["CMT"][cc - 24]
                            P.dma([("act", dst[:, row0 + j * BT: row0 + (j + 1) * BT], xcb[ci][:, 0:BT])],
                                  [BB("xcb", ci)], [BB("FT", cc, row0, j)], BB("xcb", ci))
                rr = row0 + j * BT
                P.dma([("act", S["XH"][rr:rr + BT, :].rearrange("(q p) f -> p q f", p=128), xh_tok[:, 0:nq, 0:2048]),
                       ("act", S["BMK"][rr:rr + BT, :].rearrange("(q p) f -> p q f", p=128), xh_tok[:, 0:nq, 2048:3072])],
                      [BB("xh_tok")], [BB("XHBMK", row0, j)], BB("xh_tok"))
                if flush:
                    continue
                for kind in ("v", "z"):
                    if not fl[kind]:
                        continue
                    cb0 = 0 if kind == "v" else 4
                    for cbk in range(4):
                        wi = load_w((cb0 + cbk) * 512)
                        for tq in range(nq):
                            ai = cnt["acc"] % 4
                            cnt["acc"] += 1
                            for kc in range(32):
                                P.op("pe", lambda e, ai=ai, wi=wi, kc=kc, tq=tq: e.matmul(
                                    acc[ai][:, :], hT[:, kc, tq * 128:(tq + 1) * 128], wt[wi][:, kc, :],
                                    start=(kc == 0), stop=(kc == 31)),
                                    [BB("wt", wi), BB("hT", tq)], [BB("acc", ai)])
                            if kind == "v":
                                P.op("act", lambda e, ai=ai, tq=tq, cbk=cbk: e.copy(
                                    vz_tok[:, tq, cbk * 512:(cbk + 1) * 512], acc[ai][:, :]),
                                    [BB("acc", ai)], [BB("vz_tok")])
                            else:
                                P.op("act", lambda e, ai=ai, tq=tq, cbk=cbk: e.activation(
                                    vz_tok[:, tq, cbk * 512:(cbk + 1) * 512], acc[ai][:, :], AF.Silu),
                                    [BB("acc", ai)], [BB("vz_tok")])
                    dst = S["VV"] if kind == "v" else S["SZ"]
                    P.dma([("act", dst[j * BT:(j + 1) * BT, :].rearrange("(q p) f -> p q f", p=128), vz_tok[:, 0:nq, :])],
                          [BB("vz_tok")], [BB(kind, j)], BB("vz_tok"))
                for tq in range(nq):
                    ai = cnt["acc"] % 4
                    cnt["acc"] += 1
                    for kc in range(32):
                        P.op("pe", lambda e, ai=ai, kc=kc, tq=tq: e.matmul(
                            acc[ai][:, 0:64], hT[:, kc, tq * 128:(tq + 1) * 128], wdt[:, kc, :],
                            start=(kc == 0), stop=(kc == 31)), [BB("wdt"), BB("hT", tq)], [BB("acc", ai)])
                    P.op("dve", lambda e, ai=ai: e.tensor_tensor(d1[:], acc[ai][:, 0:64], dtb[:], ALU.add),
                         [BB("acc", ai), BB("dtb")], [BB("d1")])
                    P.op("act", lambda e: e.activation(d2[:], d1[:], AF.Exp), [BB("d1")], [BB("d2")])
                    P.op("act", lambda e: e.activation(d1[:], d2[:], AF.Ln, bias=1.0), [BB("d2"), BB("d1")], [BB("d1")])
                    r = dtrow0 + j * BT + tq * 128
                    P.dma([("act", S["DT"][r:r + 128, :], d1[:])], [BB("d1")], [BB("DT", r)], BB("d1"))

        run_seq(I["ctxl"], 256, 256, 2, 4, 4608, NT, lambda j: {"cm": False, "v": False, "z": False})
        run_seq(I["xl"], NT, 512, 0, 0, 0, 0,
                lambda j: {"cm": j < 5, "v": j < 5, "z": j < 4})


def phase_ssd(K):
    nc, P, I, S, BB = K.nc, K.P, K.I, K.S, K.BB
    tri = K.tri
    with contextlib.ExitStack() as ph:
        sb = lambda n, s, dt: ph.enter_context(nc.sbuf_tensor("s_" + n, s, dt))
        L = []
        for i in range(2):
            L.append({"xh": sb("xh%d" % i, [128, 2048], BF16), "bm": sb("bm%d" % i, [128, 1024], BF16),
                      "dt": sb("dt%d" % i, [128, 64], F32), "bmT": sb("bmT%d" % i, [128, 8, 128], BF16),
                      "cmT": sb("cmT%d" % i, [128, 8, 128], BF16), "sz": sb("sz%d" % i, [128, 2048], BF16),
                      "hfb": sb("hfb%d" % i, [128, 2048], BF16)})
        rhsD = [sb("rhsD%d" % d, [128, 32, 128], F32) for d in range(2)]
        MT = [sb("MT%d" % d, [128, 32, 128], BF16) for d in range(2)]
        E32 = [sb("E32_%d" % i, [128, 512], F32) for i in range(2)]
        CBm = [sb("CBm%d" % d, [128, 8, 128], F32) for d in range(2)]
        xdt = [sb("xdt%d" % d, [128, 2048], BF16) for d in range(2)]
        xw_s = [sb("xw%d" % i, [128, 2048], BF16) for i in range(2)]
        yacc = sb("yacc", [128, 2048], F32)
        tmp = sb("tmp", [128, 2048], F32)
        H = [sb("H%d" % d, [128, 2048], F32) for d in range(2)]
        Hb = [sb("Hb%d" % d, [128, 2048], BF16) for d in range(2)]
        ssdn = sb("ssdn", [128, 2048], F32)
        mixsb = sb("mixsb", [128, 16, 128], BF16)
        a_bc = sb("a_bc", [128, 64], F32)
        dsk = sb("dsk", [128, 32], F32)
        dA_s = [sb("dA%d" % i, [128, 64], F32) for i in range(2)]
        css_s = [sb("css%d" % i, [128, 128], F32) for i in range(2)]
        ea_s = [sb("ea%d" % i, [128, 64], F32) for i in range(2)]
        t64_s = [sb("t64%d" % i, [128, 64], F32) for i in range(2)]
        wgt_s = [sb("wgt%d" % i, [128, 64], F32) for i in range(2)]
        cd_s = [sb("cd%d" % i, [128, 64], F32) for i in range(2)]
        ss = sb("ss", [128, 4], F32)
        py = ph.enter_context(nc.psum_tensor("s_py", [128, 2048], F32))
        pz = ph.enter_context(nc.psum_tensor("s_pz", [128, 2048], F32))
        print("ssd sbuf remaining", nc.sbuf_bytes_remaining)
        PZ = [BB("pz", i) for i in range(4)]

        P.dma([("sp", ssdn[:], I["ssd_norm"][0:1, :].partition_broadcast(128))], [], [BB("ssdn")], BB("ssdn"))
        P.dma([("sp", dsk[:], I["d_skip"][0:1, :].partition_broadcast(128))], [], [BB("dsk")], BB("dsk"))
        P.dma([("sp", a_bc[:], I["a_log"][0:1, :].partition_broadcast(128))], [], [BB("a_bc")], BB("a_bc"))
        P.op("act", lambda e: e.activation(a_bc[:], a_bc[:], AF.Exp), [BB("a_bc")], [BB("a_bc")])
        P.op("dve", lambda e: e.tensor_scalar(a_bc[:], a_bc[:], -1.0, None, ALU.mult), [BB("a_bc")], [BB("a_bc")])
        for d in range(2):
            P.op("dve", lambda e, d=d: e.memset(H[d][:], 0.0), [], [BB("H", d)])
        cnt = {"l": 0, "e": 0}

        def bc_hp(ap32):
            return ap32.unsqueeze(2).to_broadcast([128, 32, 64])

        def v_hp(ap):
            return ap.rearrange("p (h q) -> p h q", q=64)

        def load(c, base, dtbase, full):
            i = cnt["l"] % 2
            cnt["l"] += 1
            T = L[i]
            r = base + 128 * c + 2
            items = [("sp", T["xh"][:], S["XH"][r:r + 128, :]), ("act", T["bm"][:], S["BMK"][r:r + 128, :]),
                     ("sp", T["dt"][:], S["DT"][dtbase + 128 * c: dtbase + 128 * c + 128, :])]
            wr = [BB("L", i, "xh"), BB("L", i, "bm"), BB("L", i, "dt")]
            if full:
                items += [("act", T["bmT"][:], S["BMT"][:, :, r:r + 128].rearrange("g n t -> n g t")),
                          ("sp", T["cmT"][:], S["CMT"][:, :, r:r + 128].rearrange("g n t -> n g t")),
                          ("act", T["sz"][:], S["SZ"][128 * c:128 * c + 128, :]),
                          ("sp", T["hfb"][:], S["HFS"][c])]
                wr += [BB("L", i, "bmT"), BB("L", i, "cmT"), BB("L", i, "sz"), BB("L", i, "hfb")]
            P.dma(items, [BB("HFS", c)] if full else [], wr, BB("L", i, "k"))
            return i

        def derive(i):
            T = L[i]
            dA, css, ea, t64, wgt, cd = dA_s[i], css_s[i], ea_s[i], t64_s[i], wgt_s[i], cd_s[i]
            P.op("dve", lambda e: e.tensor_tensor(dA[:], T["dt"][:], a_bc[:], ALU.mult),
                 [BB("L", i, "dt"), BB("a_bc")], [BB("dA", i)])
            P.op("pe", lambda e: e.matmul(pz[:, 0:32], tri[:, 0, :], dA[:, 0:32], start=True, stop=True),
                 [BB("dA", i), BB("tri")], [PZ[0]])
            P.op("pe", lambda e: e.matmul(pz[:, 32:64], tri[:, 1, :], dA[:, 32:64], start=True, stop=True),
                 [BB("dA", i), BB("tri")], [PZ[0]])
            P.op("pe", lambda e: e.matmul(pz[:, 64:128], tri[:, 4, :], dA[:, 0:64], start=True, stop=True),
                 [BB("dA", i), BB("tri")], [PZ[0]])
            P.op("dve", lambda e: e.tensor_copy(css[:], pz[:, 0:128]), [PZ[0]], [BB("css", i)])
            P.op("act", lambda e: e.activation(ea[:], css[:, 0:64], AF.Exp), [BB("css", i)], [BB("ea", i)])
            P.op("dve", lambda e: e.tensor_tensor(t64[:], css[:, 64:128], css[:, 0:64], ALU.subtract),
                 [BB("css", i)], [BB("t64", i)])
            P.op("act", lambda e: e.activation(t64[:], t64[:], AF.Exp), [BB("t64", i)], [BB("t64", i)])
            P.op("dve", lambda e: e.tensor_tensor(wgt[:], t64[:], T["dt"][:], ALU.mult),
                 [BB("t64", i), BB("L", i, "dt")], [BB("wgt", i)])
            P.op("act", lambda e: e.activation(cd[:], css[:, 64:128], AF.Exp), [BB("css", i)], [BB("cd", i)])

        def update(i, d):
            T = L[i]
            wgt, cd, xw = wgt_s[i], cd_s[i], xw_s[i]
            P.op("pool", lambda e: e.tensor_tensor(v_hp(xw[:]), v_hp(T["xh"][:]), bc_hp(wgt[:, 32 * d:32 * d + 32]), ALU.mult),
                 [BB("L", i, "xh"), BB("wgt", i)], [BB("xw", i)])
            for g in range(8):
                P.op("pe", lambda e, g=g: e.matmul(pz[:, g * 256:(g + 1) * 256], T["bm"][:, g * 128:(g + 1) * 128],
                                                   xw[:, g * 256:(g + 1) * 256], start=True, stop=True),
                     [BB("L", i, "bm"), BB("xw", i)], [PZ[g // 2]])
            P.op("dve", lambda e: e.tensor_tensor(v_hp(H[d][:]), v_hp(H[d][:]), bc_hp(cd[:, 32 * d:32 * d + 32]), ALU.mult),
                 [BB("H", d), BB("cd", i)], [BB("H", d)])
            P.op("dve", lambda e: e.tensor_tensor(H[d][:], H[d][:], pz[:, :], ALU.add), [BB("H", d)] + PZ, [BB("H", d)])
            P.op("act", lambda e: e.copy(Hb[d][:], H[d][:]), [BB("H", d)], [BB("Hb", d)])

        for c in (0, 1):
            i = load(c, 4608, NT, False)
            derive(i)
            update(i, 0)
        for c in (1, 0):
            i = load(c, 4608, NT, False)
            derive(i)
            update(i, 1)
        for d in range(2):
            P.dma([("sp", S["HCTX"][d], H[d][:])], [BB("H", d)], [BB("HCTX", d)], BB("H", d))
        P.dma([("sp", S["HFS"][0], Hb[0][:])], [BB("Hb", 0)], [BB("HFS", 0)], BB("Hb", 0))
        for c in range(15):
            i = load(c, 0, 0, False)
            derive(i)
            update(i, 0)
            P.dma([("sp", S["HFS"][c + 1], Hb[0][:])], [BB("Hb", 0)], [BB("HFS", c + 1)], BB("Hb", 0))
        for c in range(31, 15, -1):
            i = load(c, 0, 0, False)
            derive(i)
            update(i, 1)
        def full_chunk(c):
            i = load(c, 0, 0, True)
            T = L[i]
            derive(i)
            dA, ea = dA_s[i], ea_s[i]
            for g in range(8):
                P.op("pe", lambda e, g=g: e.matmul(pz[:, 512 + g * 128:512 + (g + 1) * 128], T["bmT"][:, g, :], T["cmT"][:, g, :],
                                                   start=True, stop=True),
                     [BB("L", i, "bmT"), BB("L", i, "cmT")], [PZ[1 + g // 4]])
            for d in range(2):
                P.op("dve", lambda e, d=d: e.tensor_tensor(
                    CBm[d][:], pz[:, 512:1536].rearrange("p (g t) -> p g t", t=128),
                    tri[:, d, :].unsqueeze(1).to_broadcast([128, 8, 128]), ALU.mult),
                    [PZ[1], PZ[2], BB("tri")], [BB("CBm", d)])
                P.op("pool", lambda e, d=d: e.tensor_tensor(
                    rhsD[d][:], tri[:, d, :].unsqueeze(1).to_broadcast([128, 32, 128]),
                    dA[:, 32 * d:32 * d + 32].unsqueeze(2).to_broadcast([128, 32, 128]), ALU.mult),
                    [BB("dA", i), BB("tri")], [BB("rhsD", d)])
                P.op("pool", lambda e, d=d: e.tensor_tensor(
                    v_hp(xdt[d][:]), v_hp(T["xh"][:]), bc_hp(T["dt"][:, 32 * d:32 * d + 32]), ALU.mult),
                    [BB("L", i, "xh"), BB("L", i, "dt")], [BB("xdt", d)])
            for d in range(2):
                for g in range(8):
                    ei = cnt["e"] % 2
                    cnt["e"] += 1
                    bank = 0 if ei == 0 else 3
                    P.op("pe", lambda e, d=d, g=g, bank=bank: e.matmul(
                        pz[:, bank * 512:(bank + 1) * 512], tri[:, 2 + d, :],
                        rhsD[d][:, 4 * g:4 * g + 4, :], start=True, stop=True),
                        [BB("rhsD", d), BB("tri")], [PZ[bank]])
                    P.op("act", lambda e, ei=ei, bank=bank: e.activation(E32[ei][:], pz[:, bank * 512:(bank + 1) * 512], AF.Exp),
                         [PZ[bank]], [BB("E32", ei)])
                    P.op("dve", lambda e, d=d, g=g, ei=ei: e.tensor_tensor(
                        MT[d][:, 4 * g:4 * g + 4, :], E32[ei][:].rearrange("p (r t) -> p r t", t=128),
                        CBm[d][:, g, :].unsqueeze(1).to_broadcast([128, 4, 128]), ALU.mult),
                        [BB("E32", ei), BB("CBm", d)], [BB("MT", d)])
            for h in range(32):
                P.op("pe", lambda e, h=h: e.matmul(py[:, h * 64:(h + 1) * 64], MT[0][:, h, :], xdt[0][:, h * 64:(h + 1) * 64],
                                                   start=True, stop=False), [BB("MT", 0), BB("xdt", 0)], [BB("py")])
                P.op("pe", lambda e, h=h: e.matmul(py[:, h * 64:(h + 1) * 64], MT[1][:, h, :], xdt[1][:, h * 64:(h + 1) * 64],
                                                   start=False, stop=True), [BB("MT", 1), BB("xdt", 1)], [BB("py")])
            for g in range(8):
                P.op("pe", lambda e, g=g: e.matmul(pz[:, g * 256:(g + 1) * 256], T["cmT"][:, g, :], T["hfb"][:, g * 256:(g + 1) * 256],
                                                   start=True, stop=True), [BB("L", i, "cmT"), BB("L", i, "hfb")], [PZ[g // 2]])
            P.op("dve", lambda e: e.tensor_tensor(v_hp(yacc[:]), v_hp(pz[:, :]), bc_hp(ea[:, 0:32]), ALU.mult),
                 PZ + [BB("ea", i)], [BB("yacc")])
            P.op("dve", lambda e: e.tensor_tensor(yacc[:], yacc[:], py[:, :], ALU.add), [BB("yacc"), BB("py")], [BB("yacc")])
            for g in range(8):
                P.op("pe", lambda e, g=g: e.matmul(pz[:, g * 256:(g + 1) * 256], T["cmT"][:, g, :], Hb[1][:, g * 256:(g + 1) * 256],
                                                   start=True, stop=True), [BB("L", i, "cmT"), BB("Hb", 1)], [PZ[g // 2]])
            P.op("dve", lambda e: e.tensor_tensor(v_hp(tmp[:]), v_hp(pz[:, :]), bc_hp(ea[:, 32:64]), ALU.mult),
                 PZ + [BB("ea", i)], [BB("tmp")])
            P.op("dve", lambda e: e.tensor_tensor(yacc[:], yacc[:], tmp[:], ALU.add), [BB("yacc"), BB("tmp")], [BB("yacc")])
            P.op("pool", lambda e: e.tensor_tensor(v_hp(tmp[:]), v_hp(T["xh"][:]), bc_hp(dsk[:]), ALU.mult),
                 [BB("L", i, "xh"), BB("dsk"), BB("tmp")], [BB("tmp")])
            P.op("dve", lambda e: e.tensor_tensor(yacc[:], yacc[:], tmp[:], ALU.add), [BB("yacc"), BB("tmp")], [BB("yacc")])
            P.op("dve", lambda e: e.tensor_tensor(yacc[:], yacc[:], T["sz"][:], ALU.mult), [BB("yacc"), BB("L", i, "sz")], [BB("yacc")])
            P.op("dve", lambda e: e.memset(ss[:], 0.0), [], [BB("ss")])
            P.op("act", lambda e: e.activation(tmp[:], yacc[:], AF.Square, accum_out=ss[:, 0:1]),
                 [BB("yacc"), BB("ss"), BB("tmp")], [BB("tmp"), BB("ss")])
            P.op("dve", lambda e: e.tensor_scalar(ss[:, 1:2], ss[:, 0:1], 1.0 / 2048, EPS, ALU.mult, ALU.add), [BB("ss")], [BB("ss")])
            P.op("act", lambda e: e.activation(ss[:, 2:3], ss[:, 1:2], AF.Sqrt), [BB("ss")], [BB("ss")])
            P.op("dve", lambda e: e.reciprocal(ss[:, 3:4], ss[:, 2:3]), [BB("ss")], [BB("ss")])
            P.op("dve", lambda e: e.scalar_tensor_tensor(tmp[:], yacc[:], ss[:, 3:4], ssdn[:], ALU.mult, ALU.mult),
                 [BB("yacc"), BB("ss"), BB("ssdn"), BB("tmp")], [BB("tmp")])
            for fc in range(16):
                P.op("pe", lambda e, fc=fc: e.transpose(pz[:, fc * 128:(fc + 1) * 128], tmp[:, fc * 128:(fc + 1) * 128], K.ident),
                     [BB("tmp"), BB("tri")], [PZ[fc // 4]])
            P.op("act", lambda e: e.copy(mixsb[:], pz[:, :].rearrange("p (f t) -> p f t", t=128)), PZ, [BB("mixsb")])
            P.dma([("sp", S["MIXT"][16:32, :, c * 128:(c + 1) * 128].rearrange("fc f t -> f fc t"), mixsb[:])],
                  [BB("mixsb")], [BB("MIXT", "s", c)], BB("mixsb"))
            update(i, 1)

        for c in range(15, -1, -1):
            full_chunk(c)


def phase_pool(K):
    nc, P, I, S, BB = K.nc, K.P, K.I, K.S, K.BB
    K.stage_late2()
    slow = {"allow_slow_non_contiguous": True}
    with contextlib.ExitStack() as ph:
        sb = lambda n, s, dt: ph.enter_context(nc.sbuf_tensor("p_" + n, s, dt))
        vch = [sb("vch%d" % i, [128, 2048], BF16) for i in range(10)]
        PM = sb("PM", [128, 4, 9, 128], BF16)
        invc = sb("invc", [128, 16, 4], F32)
        poolw = sb("poolw", [128, 4, 4, 512], BF16)
        psc = sb("psc", [128, 16], F32)
        diff32s = [sb("diff32_%d" % i, [128, 2048], F32) for i in range(2)]
        diffTs = [sb("diffT%d" % i, [128, 16, 128], BF16) for i in range(2)]
        mixsbs = [sb("mixsb%d" % i, [128, 16, 128], BF16) for i in range(2)]
        pq = ph.enter_context(nc.psum_tensor("p_pq", [128, 2048], F32))
        ptr = ph.enter_context(nc.psum_tensor("p_ptr", [128, 2048], F32))
        PQ = [BB("pq", i) for i in range(4)]
        PT = [BB("ptr", i) for i in range(4)]
        P.dma([("pool", PM[:], I["pm"])], [], [BB("PM")], BB("PM"))
        P.dma([("sp", invc[:], I["invc"])], [], [BB("invc")], BB("invc"))
        for g in range(4):
            P.dma([("act", poolw[:, g], S["poolwB"][g].rearrange("(cci p) d -> p cci d", p=128))],
                  [BB("poolwB", 0)], [BB("poolw", g)], BB("poolw", g))
        P.dma([("sp", psc[:], I["pool_scale"][0, :].rearrange("(c p) -> p c", p=128), slow)], [], [BB("psc")], BB("psc"))
        loaded = set()

        def need(k):
            if k in loaded:
                return
            loaded.add(k)
            P.dma([("sp", vch[k % 10][:], S["VV"][k * 128:(k + 1) * 128, :])],
                  [], [BB("vch", k % 10)], BB("vch", k % 10))

        def chunk(o):
            ob = o % 2
            diff32, diffT, mixsb = diff32s[ob], diffTs[ob], mixsbs[ob]
            for k in range(max(0, o - 4), min(19, o + 4) + 1):
                need(k)
            for w in range(4):
                nd = POOL_ND[w]
                ds = [d for d in range(-nd, nd + 1) if 0 <= o + d <= 19]
                for idx, d in enumerate(ds):
                    P.op("pe", lambda e, w=w, d=d, idx=idx, n=len(ds): e.matmul(
                        pq[:, w * 512:(w + 1) * 512], PM[:, w, d + 4, :], vch[(o + d) % 10][:, w * 512:(w + 1) * 512],
                        start=(idx == 0), stop=(idx == n - 1)), [BB("PM"), BB("vch", (o + d) % 10)], [PQ[w]])
                P.op("dve", lambda e, w=w: e.scalar_tensor_tensor(
                    diff32[:, w * 512:(w + 1) * 512], pq[:, w * 512:(w + 1) * 512], invc[:, o, w:w + 1],
                    vch[o % 10][:, w * 512:(w + 1) * 512], ALU.mult, ALU.subtract),
                    [PQ[w], BB("invc"), BB("vch", o % 10)], [BB("diff32", ob)])
            for cc in range(16):
                P.op("pe", lambda e, cc=cc: e.transpose(ptr[:, cc * 128:(cc + 1) * 128], diff32[:, cc * 128:(cc + 1) * 128], K.ident),
                     [BB("diff32", ob), BB("tri")], [PT[cc // 4]])
            P.op("act", lambda e: e.copy(diffT[:], ptr[:, :].rearrange("p (c t) -> p c t", t=128)), PT, [BB("diffT", ob)])
            for g in range(4):
                for dcc in range(4):
                    fc = 4 * g + dcc
                    for cci in range(4):
                        P.op("pe", lambda e, g=g, dcc=dcc, cci=cci, fc=fc: e.matmul(
                            pq[:, fc * 128:(fc + 1) * 128], poolw[:, g, cci, dcc * 128:(dcc + 1) * 128], diffT[:, 4 * g + cci, :],
                            start=(cci == 0), stop=(cci == 3)), [BB("poolw", g), BB("diffT", ob)], [PQ[fc // 4]])
                    P.op("act", lambda e, fc=fc: e.activation(mixsb[:, fc, :], pq[:, fc * 128:(fc + 1) * 128], AF.Copy,
                                                              scale=psc[:, fc:fc + 1]),
                         [PQ[fc // 4], BB("psc")], [BB("mixsb", ob)])
            P.dma([("sp", S["MIXT"][0:16, :, o * 128:(o + 1) * 128].rearrange("fc f t -> f fc t"), mixsb[:])],
                  [BB("mixsb", ob)], [BB("MIXT", "p", o)], BB("mixsb", ob))

        for o in range(16):
            chunk(o)


def phase_wout(K):
    nc, P, I, S, BB = K.nc, K.P, K.I, K.S, K.BB
    with contextlib.ExitStack() as ph:
        sb = lambda n, s, dt: ph.enter_context(nc.sbuf_tensor("o_" + n, s, dt))
        mixT = sb("mixT", [128, 32, 256], BF16)
        wt = [sb("wt%d" % i, [128, 8, 2048], BF16) for i in range(2)]
        xin = [sb("xin%d" % i, [128, D], F32) for i in range(2)]
        g1b = sb("g1b", [128, D], F32)
        tmpo = [sb("tmpo%d" % i, [128, 512], F32) for i in range(2)]
        acc = [ph.enter_context(nc.psum_tensor("o_acc%d" % i, [128, 512], F32)) for i in range(8)]
        P.dma([("sp", g1b[:], S["modrow"][0:1, 2 * D:3 * D].partition_broadcast(128))], [], [BB("g1b")], BB("g1b"))

        cnt = {"w": 0, "t": 0}

        def block(tb):
            split_dma(P, mixT[:], S["MIXT"][:, :, tb * 256:(tb + 1) * 256].rearrange("fc f t -> f fc t"),
                      [], [BB("mixT")], BB("mixT"), n=2, axis=1)
            for tq in range(2):
                r = tb * 256 + tq * 128
                split_dma(P, xin[tq][:], I["xl"][r:r + 128, :], [], [BB("xin", tq)], BB("xin", tq), n=2, axis=1)
            for dmh in range(2):
                for fcg in range(4):
                    wi = cnt["w"] % 2
                    cnt["w"] += 1
                    P.dma([("sp", wt[wi][:], S["w_outB"][dmh, fcg].rearrange("p (a n) -> p a n", n=2048))],
                          [BB("w_outB", dmh, fcg)], [BB("wt", wi)], BB("wt", wi))
                    for fci in range(8):
                        fc = fcg * 8 + fci
                        for tq in range(2):
                            for dmb in range(4):
                                P.op("pe", lambda e, wi=wi, fci=fci, fc=fc, tq=tq, dmb=dmb: e.matmul(
                                    acc[tq * 4 + dmb][:, :], mixT[:, fc, tq * 128:(tq + 1) * 128],
                                    wt[wi][:, fci, dmb * 512:(dmb + 1) * 512], start=(fc == 0), stop=(fc == 31)),
                                    [BB("mixT"), BB("wt", wi)], [BB("acc", tq * 4 + dmb)])
                for tq in range(2):
                    for dmb in range(4):
                        c0 = dmh * 2048 + dmb * 512
                        ti = cnt["t"] % 2
                        cnt["t"] += 1
                        P.op("dve", lambda e, tq=tq, dmb=dmb, c0=c0, ti=ti: e.tensor_tensor(
                            tmpo[ti][:], acc[tq * 4 + dmb][:, :], g1b[:, c0:c0 + 512], ALU.mult),
                            [BB("acc", tq * 4 + dmb), BB("g1b")], [BB("tmpo", ti)])
                        P.op("pool", lambda e, tq=tq, c0=c0, ti=ti: e.tensor_tensor(
                            xin[tq][:, c0:c0 + 512], xin[tq][:, c0:c0 + 512], tmpo[ti][:], ALU.add),
                            [BB("tmpo", ti), BB("xin", tq)], [BB("xin", tq)])
            for tq in range(2):
                r = tb * 256 + tq * 128
                P.dma([("act", S["X1"][r:r + 128, :], xin[tq][:])], [BB("xin", tq)], [BB("X1", r)], BB("xin", tq))

        for tb in range(8):
            block(tb)


def phase_peer_prep(K):
    nc, P, I, S, BB = K.nc, K.P, K.I, K.S, K.BB
    with contextlib.ExitStack() as ph:
        sb = lambda n, s, dt: ph.enter_context(nc.sbuf_tensor("q_" + n, s, dt))
        xt = sb("xt", [128, D], F32)
        junk = sb("junk", [128, D], BF16)
        ss = sb("ss", [128, 4], F32)
        hT = sb("hT", [128, 32, 512], BF16)
        wt = [sb("wt%d" % i, [128, 32, 256], BF16) for i in range(2)]
        qT = sb("qT", [128, 16, 512], F32)
        kin = sb("kin", [128, 16, 128], F32)
        keysT = sb("keysT", [128, 16, 128], F32)
        s_sb = [sb("s_sb%d" % i, [128, 16, 128], F32) for i in range(2)]
        s2 = sb("s2", [128, 16, 128], F32)
        m16 = sb("m16", [128, 16, 16], F32)
        cand = sb("cand", [128, 8, 256], F32)
        cand2 = sb("cand2", [128, 8, 256], F32)
        g16 = sb("g16", [128, 8, 16], F32)
        g24 = sb("g24", [128, 8, 8], F32)
        th = sb("th", [128, 8], F32)
        th2 = sb("th2", [128, 8], F32)
        s3 = sb("s3", [128, 16, 128], F32)
        m24 = sb("m24", [128, 16, 8], F32)
        e16 = sb("e16", [128, 8, 16], F32)
        zz = sb("zz", [128, 8], F32)
        tn = [sb("tn%d" % i, [128, 16], F32) for i in range(2)]
        tp = [ph.enter_context(nc.psum_tensor("q_tp%d" % i, [128, 512], F32)) for i in range(2)]
        acc = [ph.enter_context(nc.psum_tensor("q_acc%d" % i, [128, 512], F32)) for i in range(4)]
        psc = [ph.enter_context(nc.psum_tensor("q_psc%d" % i, [128, 512], F32)) for i in range(2)]
        print("peer_prep sbuf remaining", nc.sbuf_bytes_remaining)
        cnt = {"wt": 0, "acc": 0, "tp": 0, "ps": 0, "s": 0}
        P.dma([("sp", kin[:], I["peer_keys"].rearrange("hs e k -> e hs k"))], [], [BB("kin")], BB("kin"))
        for hs in range(16):
            pi = hs // 4 % 2
            P.op("pe", lambda e, hs=hs, pi=pi: e.transpose(psc[pi][:, (hs % 4) * 128:(hs % 4 + 1) * 128], kin[:, hs, :], K.ident),
                 [BB("kin"), BB("tri")], [BB("psc", pi)])
            if hs % 4 == 3:
                P.op("dve", lambda e, hs=hs, pi=pi: e.tensor_copy(
                    keysT[:, hs - 3:hs + 1, :], psc[pi][:, :].rearrange("p (a t) -> p a t", t=128)),
                    [BB("psc", pi)], [BB("keysT")])

        def block(j):
            for tq in range(4):
                r = j * 512 + tq * 128
                split_dma(P, xt[:], S["X1"][r:r + 128, :], [], [BB("xt")], BB("xt"), n=2, axis=1)
                P.op("dve", lambda e: e.memset(ss[:], 0.0), [], [BB("ss")])
                P.op("act", lambda e: e.activation(junk[:], xt[:], AF.Square, accum_out=ss[:, 0:1]),
                     [BB("xt"), BB("ss")], [BB("junk"), BB("ss")])
                P.op("dve", lambda e: e.tensor_scalar(ss[:, 1:2], ss[:, 0:1], 1.0 / D, EPS, ALU.mult, ALU.add), [BB("ss")], [BB("ss")])
                P.op("act", lambda e: e.activation(ss[:, 2:3], ss[:, 1:2], AF.Sqrt), [BB("ss")], [BB("ss")])
                P.op("dve", lambda e: e.reciprocal(ss[:, 3:4], ss[:, 2:3]), [BB("ss")], [BB("ss")])
                P.op("dve", lambda e: e.tensor_scalar(xt[:], xt[:], ss[:, 3:4], None, ALU.mult), [BB("xt"), BB("ss")], [BB("xt")])
                for kc in range(32):
                    if kc % 4 == 0:
                        ti = cnt["tp"] % 2
                        cnt["tp"] += 1
                    P.op("pe", lambda e, ti=ti, kc=kc: e.transpose(
                        tp[ti][:, (kc % 4) * 128:(kc % 4 + 1) * 128], xt[:, kc * 128:(kc + 1) * 128], K.ident),
                        [BB("xt"), BB("tri")], [BB("tp", ti)])
                    if kc % 4 == 3:
                        for k2 in range(kc - 3, kc + 1):
                            P.op("act", lambda e, ti=ti, k2=k2, tq=tq: e.activation(
                                hT[:, k2, tq * 128:(tq + 1) * 128], tp[ti][:, (k2 % 4) * 128:(k2 % 4 + 1) * 128],
                                AF.Identity, bias=K.colv[:, 2, k2:k2 + 1], scale=K.gam[:, 1, k2:k2 + 1]),
                                [BB("tp", ti), BB("gam"), BB("colv", 2)], [BB("hT", tq)])
            hT_bufs = [BB("hT", tq) for tq in range(4)]
            P.dma([("act", S["H2T"][:, :, j * 512:(j + 1) * 512].rearrange("kc p t -> p kc t"), hT[:])],
                  hT_bufs, [BB("H2T", j)], BB("hT", "k"))
            for cbk in range(8):
                wi = cnt["wt"] % 2
                cnt["wt"] += 1
                P.dma([("sp", wt[wi][:], S["wqB"][cbk].rearrange("p (kc n) -> p kc n", n=256))],
                      [BB("wqB", cbk)], [BB("wt", wi)], BB("wt", wi))
                for sub in range(2):
                    hs = cbk * 2 + sub
                    ai = cnt["acc"] % 4
                    cnt["acc"] += 1
                    for kc in range(32):
                        P.op("pe", lambda e, ai=ai, wi=wi, kc=kc, sub=sub: e.matmul(
                            acc[ai][:, :], wt[wi][:, kc, sub * 128:(sub + 1) * 128], hT[:, kc, :],
                            start=(kc == 0), stop=(kc == 31)), [BB("wt", wi)] + hT_bufs, [BB("acc", ai)])
                    P.op("dve", lambda e, ai=ai, hs=hs: e.tensor_copy(qT[:, hs, :], acc[ai][:, :]), [BB("acc", ai)], [BB("qT")])
            for tq in range(4):
                si = cnt["s"] % 2
                cnt["s"] += 1
                for hs in range(16):
                    if hs % 4 == 0:
                        pi = cnt["ps"] % 2
                        cnt["ps"] += 1
                    P.op("pe", lambda e, pi=pi, hs=hs, tq=tq: e.matmul(
                        psc[pi][:, (hs % 4) * 128:(hs % 4 + 1) * 128], qT[:, hs, tq * 128:(tq + 1) * 128], keysT[:, hs, :],
                        start=True, stop=True), [BB("qT"), BB("keysT")], [BB("psc", pi)])
                    if hs % 4 == 3:
                        P.op("act", lambda e, pi=pi, hs=hs, si=si: e.copy(
                            s_sb[si][:, hs - 3:hs + 1, :], psc[pi][:, :].rearrange("p (a t) -> p a t", t=128)),
                            [BB("psc", pi)], [BB("s_sb", si)])
                r = j * 512 + tq * 128
                P.dma([("sp", S["SS"][r:r + 128, :], s_sb[si][:].rearrange("p a t -> p (a t)"))],
                      [BB("s_sb", si)], [BB("SS", r)], BB("s_sb", si))
                topk(si, r)

        def topk(si, r):
            sv = s_sb[si]
            for hs in range(16):
                P.op("dve", lambda e, hs=hs: e.max(m16[:, hs, 0:8], sv[:, hs, :]), [BB("s_sb", si)], [BB("m16")])
                P.op("dve", lambda e, hs=hs: e.match_replace(s2[:, hs, :], m16[:, hs, 0:8], sv[:, hs, :], NEG),
                     [BB("s_sb", si), BB("m16")], [BB("s2")])
                P.op("dve", lambda e, hs=hs: e.max(m16[:, hs, 8:16], s2[:, hs, :]), [BB("s2")], [BB("m16")])
                P.op("dve", lambda e, hs=hs: e.match_replace(s3[:, hs, :], m16[:, hs, 8:16], s2[:, hs, :], NEG),
                     [BB("s2"), BB("m16")], [BB("s3")])
                P.op("dve", lambda e, hs=hs: e.max(m24[:, hs, :], s3[:, hs, :]), [BB("s3")], [BB("m24")])
            m16v = m16[:].rearrange("p (h s) k -> p h s k", s=2)
            P.op("dve", lambda e: e.tensor_tensor(
                cand[:].rearrange("p h (a b) -> p h a b", a=16),
                m16v[:, :, 0, :].unsqueeze(3).to_broadcast([128, 8, 16, 16]),
                m16v[:, :, 1, :].unsqueeze(2).to_broadcast([128, 8, 16, 16]), ALU.add), [BB("m16")], [BB("cand")])
            for h in range(8):
                P.op("dve", lambda e, h=h: e.max(g16[:, h, 0:8], cand[:, h, :]), [BB("cand")], [BB("g16")])
                P.op("dve", lambda e, h=h: e.match_replace(cand2[:, h, :], g16[:, h, 0:8], cand[:, h, :], NEG),
                     [BB("cand"), BB("g16")], [BB("cand2")])
                P.op("dve", lambda e, h=h: e.max(g16[:, h, 8:16], cand2[:, h, :]), [BB("cand2")], [BB("g16")])
                P.op("dve", lambda e, h=h: e.match_replace(cand[:, h, :], g16[:, h, 8:16], cand2[:, h, :], NEG),
                     [BB("cand2"), BB("g16"), BB("cand")], [BB("cand")])
                P.op("dve", lambda e, h=h: e.max(g24[:, h, :], cand[:, h, :]), [BB("cand")], [BB("g24")])
            ti = si
            P.op("dve", lambda e: e.tensor_tensor(e16[:], g16[:], g16[:, :, 0:1].to_broadcast([128, 8, 16]), ALU.subtract),
                 [BB("g16")], [BB("e16")])
            P.op("act", lambda e: e.activation(e16[:], e16[:], AF.Exp), [BB("e16")], [BB("e16")])
            P.op("dve", lambda e: e.tensor_reduce(zz[:], e16[:], AX.X, ALU.add), [BB("e16")], [BB("zz")])
            P.op("act", lambda e: e.activation(zz[:], zz[:], AF.Ln), [BB("zz")], [BB("zz")])
            P.op("dve", lambda e: e.tensor_tensor(zz[:], zz[:], g16[:, :, 0], ALU.add), [BB("zz"), BB("g16")], [BB("zz")])
            P.op("dve", lambda e: e.tensor_scalar(tn[ti][:, 8:16], zz[:], -1.0, None, ALU.mult), [BB("zz"), BB("tn", ti)], [BB("tn", ti)])
            m24v = m24[:].rearrange("p (h s) k -> p h s k", s=2)
            P.op("dve", lambda e: e.tensor_tensor(th[:], m24v[:, :, 0, 0], m16v[:, :, 1, 0], ALU.add), [BB("m24"), BB("m16")], [BB("th")])
            P.op("dve", lambda e: e.tensor_tensor(th2[:], m16v[:, :, 0, 0], m24v[:, :, 1, 0], ALU.add), [BB("m24"), BB("m16")], [BB("th2")])
            P.op("dve", lambda e: e.tensor_tensor(th[:], th[:], th2[:], ALU.max), [BB("th"), BB("th2")], [BB("th")])
            P.op("dve", lambda e: e.tensor_tensor(th[:], th[:], g24[:, :, 0], ALU.max), [BB("th"), BB("g24")], [BB("th")])
            P.op("dve", lambda e: e.tensor_tensor(th[:], th[:], g16[:, :, 15], ALU.add), [BB("th"), BB("g16")], [BB("th")])
            P.op("dve", lambda e: e.scalar_tensor_tensor(th[:], th[:], 0.5, tn[ti][:, 8:16], ALU.mult, ALU.add),
                 [BB("th"), BB("tn", ti)], [BB("th")])
            P.op("act", lambda e: e.activation(tn[ti][:, 0:8], th[:], AF.Exp), [BB("th"), BB("tn", ti)], [BB("tn", ti)])
            P.dma([("act", S["TN"][r:r + 128, :], tn[ti][:])], [BB("tn", ti)], [BB("TN", r)], BB("tn", ti))

        for j in range(4):
            block(j)


def phase_ustage(K):
    nc, P, I, S, BB = K.nc, K.P, K.I, K.S, K.BB
    with contextlib.ExitStack() as ph:
        sb = lambda n, s, dt: ph.enter_context(nc.sbuf_tensor("u_" + n, s, dt))
        usrc_all = [sb("usrc%d" % i, [128, D], F32) for i in range(8)]
        utsb = [sb("utsb%d" % i, [128, 32, 512], BF16) for i in range(2)]
        pt = [ph.enter_context(nc.psum_tensor("u_pt%d" % i, [128, 512], F32)) for i in range(4)]

        def group(eg):
            ui = eg % 2
            usrc = usrc_all[4 * ui:4 * ui + 4]
            for et in range(4):
                r = eg * 512 + et * 128
                split_dma(P, usrc[et][:], I["peer_u"][r:r + 128, :], [], [BB("usrc", ui, et)], BB("usrc", ui, et), n=2, axis=1)
            for dc in range(32):
                pi = dc % 4
                for et in range(4):
                    P.op("pe", lambda e, pi=pi, et=et, dc=dc: e.transpose(
                        pt[pi][:, et * 128:(et + 1) * 128], usrc[et][:, dc * 128:(dc + 1) * 128], K.ident),
                        [BB("usrc", ui, et), BB("tri")], [BB("upt", pi)])
                eng = "act" if dc % 2 == 0 else "dve"
                if eng == "act":
                    P.op("act", lambda e, pi=pi, dc=dc: e.copy(utsb[ui][:, dc, :], pt[pi][:, :]), [BB("upt", pi)], [BB("utsb", ui)])
                else:
                    P.op("dve", lambda e, pi=pi, dc=dc: e.tensor_copy(utsb[ui][:, dc, :], pt[pi][:, :]), [BB("upt", pi)], [BB("utsb", ui)])
            P.dma([("sp", S["UT"][eg].rearrange("p (dc e) -> p dc e", e=512), utsb[ui][:])],
                  [BB("utsb", ui)], [BB("UT", eg)], BB("utsb", ui))

        for eg in range(32):
            group(eg)


def phase_peer_dense(K):
    nc, P, I, S, BB = K.nc, K.P, K.I, K.S, K.BB
    phase_ustage(K)
    P.barrier()
    with contextlib.ExitStack() as ph:
        sb = lambda n, s, dt: ph.enter_context(nc.sbuf_tensor("d_" + n, s, dt))
        h2T = sb("h2T", [128, 32, 256], BF16)
        s_t = [sb("s_t%d" % i, [128, 16, 128], F32) for i in range(2)]
        tnt = [sb("tnt%d" % i, [128, 16], F32) for i in range(2)]
        AT = sb("AT", [128, 128, 256], BF16)
        wtile = [sb("wtile%d" % i, [128, 16384], BF16) for i in range(2)]
        NB = 3
        bq = [sb("bq%d" % i, [128, 8, 128], F32) for i in range(2)]
        e1q = [sb("e1q%d" % i, [128, 4, 128], F32) for i in range(2)]
        e0q = [sb("e0q%d" % i, [128, 4, 128], F32) for i in range(2)]
        wvA = [sb("wvA%d" % i, [128, 4, 128], F32) for i in range(3)]
        wvD = [sb("wvD%d" % i, [128, 4, 128], F32) for i in range(2)]
        Gh = [sb("Gh%d" % i, [128, 4, 128], F32) for i in range(3)]
        G = [sb("G%d" % i, [128, 4, 128], F32) for i in range(2)]
        ge = [sb("ge%d" % i, [128, 512], F32) for i in range(2)]
        A32 = ge
        po = [sb("po%d" % i, [128, 512], F32) for i in range(1)]
        pb = [ph.enter_context(nc.psum_tensor("d_pb%d" % i, [128, 512], F32)) for i in range(8)]
        print("peer_dense sbuf remaining", nc.sbuf_bytes_remaining)
        cnt = {"w": 0, "pa": 0, "g": 0, "h": 0, "po": 0, "d": 0, "gh": 0}

        def block(tb):
            split_dma(P, h2T[:], S["H2T"][:, :, tb * 256:(tb + 1) * 256].rearrange("kc p t -> p kc t"),
                      [], [BB("h2T")], BB("h2T"), n=2, axis=1)
            for tq in range(2):
                r = tb * 256 + tq * 128
                P.dma([("sp", s_t[tq][:].rearrange("p a t -> p (a t)"), S["SS"][r:r + 128, :]),
                       ("act", tnt[tq][:], S["TN"][r:r + 128, :])], [], [BB("s_t", tq)], BB("s_t", tq))
                P.op("dve", lambda e, tq=tq: e.tensor_tensor(
                    bq[tq][:], s_t[tq][:].rearrange("p (h s) t -> p h s t", s=2)[:, :, 0, :],
                    tnt[tq][:, 8:16].unsqueeze(2).to_broadcast([128, 8, 128]), ALU.add), [BB("s_t", tq)], [BB("bq", tq)])
                P.op("act", lambda e, tq=tq: e.activation(
                    e1q[tq][:], s_t[tq][:].rearrange("p (h s) t -> p h s t", s=2)[:, 4:8, 1, :], AF.Exp),
                    [BB("s_t", tq)], [BB("e1q", tq)])
                P.op("act", lambda e, tq=tq: e.activation(e0q[tq][:], bq[tq][:, 4:8, :], AF.Exp), [BB("bq", tq)], [BB("e0q", tq)])
            pend = []
            for eb in range(32):
                wi = cnt["w"] % 2
                cnt["w"] += 1
                ut = wtile[wi][:].rearrange("p (dc e) -> p dc e", e=512)
                P.dma([("sp", wtile[wi][:], S["UT"][eb])], [], [BB("wtile", wi)], BB("wtile", wi))
                pis = []
                for tq in range(2):
                    pi = cnt["pa"] % 4
                    cnt["pa"] += 1
                    pis.append(pi)
                    for dc in range(32):
                        P.op("pe", lambda e, pi=pi, dc=dc, tq=tq, ut=ut: e.matmul(
                            pb[pi][:, :], h2T[:, dc, tq * 128:(tq + 1) * 128], ut[:, dc, :],
                            start=(dc == 0), stop=(dc == 31)), [BB("h2T"), BB("wtile", wi)], [BB("pb", pi)])
                while pend:
                    pend.pop(0)()
                for tq in range(2):
                    gi = tq
                    HA, HD = (0, 1, 2, 3, 4), (5, 6, 7)
                    wa = {}

                    def act_head(h, tq=tq, eb=eb, wa=wa):
                        ai = cnt["h"] % 3
                        cnt["h"] += 1
                        wa[h] = ai
                        for il in range(4):
                            P.op("act", lambda e, ai=ai, h=h, tq=tq, eb=eb, il=il: e.activation(
                                wvA[ai][:, il, :], s_t[tq][:, 2 * h + 1, :], AF.Exp,
                                bias=bq[tq][:, h, eb * 4 + il:eb * 4 + il + 1]), [BB("s_t", tq), BB("bq", tq)], [BB("wvA", ai)])
                    for h in HA[:3]:
                        act_head(h)
                    pending = None
                    first = True
                    for h in HD + HA:
                        if h in HD:
                            di = cnt["d"] % 2
                            cnt["d"] += 1
                            P.op("dve", lambda e, di=di, h=h, tq=tq, eb=eb: e.tensor_tensor(
                                wvD[di][:], e1q[tq][:, h - 4, :].unsqueeze(1).to_broadcast([128, 4, 128]),
                                e0q[tq][:, h - 4, eb * 4:(eb + 1) * 4].unsqueeze(2).to_broadcast([128, 4, 128]), ALU.mult),
                                [BB("e1q", tq), BB("e0q", tq)], [BB("wvD", di)])
                            wsrc, wbuf = wvD[di], BB("wvD", di)
                        else:
                            wsrc, wbuf = wvA[wa[h]], BB("wvA", wa[h])
                        if first:
                            P.op("dve", lambda e, wsrc=wsrc, h=h, tq=tq, gi=gi: e.scalar_tensor_tensor(
                                G[gi][:], wsrc[:], tnt[tq][:, h:h + 1], wsrc[:], ALU.is_ge, ALU.mult),
                                [wbuf, BB("s_t", tq)], [BB("G", gi)])
                            first = False
                            continue
                        gh = cnt["gh"] % 3
                        cnt["gh"] += 1
                        P.op("dve", lambda e, wsrc=wsrc, h=h, tq=tq, gh=gh: e.scalar_tensor_tensor(
                            Gh[gh][:], wsrc[:], tnt[tq][:, h:h + 1], wsrc[:], ALU.is_ge, ALU.mult),
                            [wbuf, BB("s_t", tq)], [BB("Gh", gh)])
                        if pending is not None:
                            P.op("dve", lambda e, hp=pending, gi=gi: e.tensor_tensor(G[gi][:], G[gi][:], Gh[hp][:], ALU.add),
                                 [BB("G", gi), BB("Gh", pending)], [BB("G", gi)])
                        pending = gh
                        if h in HA and HA.index(h) + 3 < len(HA):
                            act_head(HA[HA.index(h) + 3])
                    P.op("dve", lambda e, hp=pending, gi=gi: e.tensor_tensor(G[gi][:], G[gi][:], Gh[hp][:], ALU.add),
                         [BB("G", gi), BB("Gh", pending)], [BB("G", gi)])
                for tq in range(2):
                    gi, pi = tq, pis[tq]
                    P.op("act", lambda e, pi=pi, gi=gi: e.activation(ge[gi][:], pb[pi][:, :], AF.Gelu), [BB("pb", pi)], [BB("ge", gi)])
                for tq in range(2):
                    gi = tq
                    P.op("dve", lambda e, gi=gi: e.tensor_tensor(A32[gi][:], ge[gi][:], G[gi][:].rearrange("p a t -> p (a t)"), ALU.mult),
                         [BB("ge", gi), BB("G", gi)], [BB("ge", gi)])

                    def fin(gi=gi, eb=eb, tq=tq):
                        ti = 4 + gi
                        for sub in range(4):
                            P.op("pe", lambda e, ti=ti, gi=gi, sub=sub: e.transpose(
                                pb[ti][:, sub * 128:(sub + 1) * 128], A32[gi][:, sub * 128:(sub + 1) * 128], K.ident),
                                [BB("ge", gi), BB("tri")], [BB("pb", ti)])
                        P.op("act", lambda e, ti=ti, eb=eb, tq=tq: e.copy(
                            AT[:, eb * 4:(eb + 1) * 4, tq * 128:(tq + 1) * 128], pb[ti][:, :].rearrange("p (a t) -> p a t", t=128)),
                            [BB("pb", ti)], [BB("AT")])
                    pend.append(fin)
            while pend:
                pend.pop(0)()
            for dmh in range(2):
                for ecg in range(16):
                    wi = cnt["w"] % 2
                    cnt["w"] += 1
                    vt = wtile[wi][:].rearrange("p (a n) -> p a n", n=2048)
                    P.dma([("sp", wtile[wi][:], S["VB"][dmh, ecg])], [BB("VB", dmh, ecg)],
                          [BB("wtile", wi)], BB("wtile", wi))
                    for eci in range(8):
                        ec = ecg * 8 + eci
                        for tq in range(2):
                            for dmb in range(4):
                                P.op("pe", lambda e, vt=vt, eci=eci, ec=ec, tq=tq, dmb=dmb: e.matmul(
                                    pb[tq * 4 + dmb][:, :], AT[:, ec, tq * 128:(tq + 1) * 128], vt[:, eci, dmb * 512:(dmb + 1) * 512],
                                    start=(ec == 0), stop=(ec == 127)), [BB("AT"), BB("wtile", wi)], [BB("pb", tq * 4 + dmb)])
                for tq in range(2):
                    for dmb in range(4):
                        oi = 0
                        r = tb * 256 + tq * 128
                        c0 = dmh * 2048 + dmb * 512
                        P.op("act", lambda e, oi=oi, tq=tq, dmb=dmb: e.copy(po[oi][:], pb[tq * 4 + dmb][:, :]),
                             [BB("pb", tq * 4 + dmb)], [BB("po", oi)])
                        P.dma([("act", S["PO"][r:r + 128, c0:c0 + 512], po[oi][:])], [BB("po", oi)], [BB("PO", r, c0)], BB("po", oi))

        for tb in range(8):
            block(tb)


def phase_final(K):
    nc, P, I, S, BB = K.nc, K.P, K.I, K.S, K.BB
    with contextlib.ExitStack() as ph:
        sb = lambda n, s, dt: ph.enter_context(nc.sbuf_tensor("f_" + n, s, dt))
        x1 = [sb("x1_%d" % i, [128, D], F32) for i in range(2)]
        pot = [sb("pot%d" % i, [128, D], F32) for i in range(2)]
        g2b = sb("g2b", [128, D], F32)
        fnb = sb("fnb", [128, D], F32)
        ss = sb("ss", [128, 4], F32)
        P.dma([("sp", g2b[:], S["modrow"][0:1, 5 * D:6 * D].partition_broadcast(128))], [], [BB("g2b")], BB("g2b"))
        P.dma([("act", fnb[:], I["nrm"][2:3, :].partition_broadcast(128))], [], [BB("fnb")], BB("fnb"))

        def chunk(c):
            i = c % 2
            r = c * 128
            split_dma(P, x1[i][:], S["X1"][r:r + 128, :], [], [BB("x1", i)], BB("x1", i), n=2, axis=1)
            split_dma(P, pot[i][:], S["PO"][r:r + 128, :], [], [BB("pot", i)], BB("pot", i), n=2, axis=1)
            P.op("dve", lambda e: e.tensor_tensor(pot[i][:], pot[i][:], g2b[:], ALU.mult), [BB("pot", i), BB("g2b")], [BB("pot", i)])
            P.op("pool", lambda e: e.tensor_tensor(x1[i][:], x1[i][:], pot[i][:], ALU.add), [BB("pot", i), BB("x1", i)], [BB("x1", i)])
            P.op("dve", lambda e: e.memset(ss[:], 0.0), [], [BB("ss")])
            P.op("act", lambda e: e.activation(pot[i][:], x1[i][:], AF.Square, accum_out=ss[:, 0:1]),
                 [BB("x1", i), BB("ss"), BB("pot", i)], [BB("pot", i), BB("ss")])
            P.op("dve", lambda e: e.tensor_scalar(ss[:, 1:2], ss[:, 0:1], 1.0 / D, EPS, ALU.mult, ALU.add), [BB("ss")], [BB("ss")])
            P.op("act", lambda e: e.activation(ss[:, 2:3], ss[:, 1:2], AF.Sqrt), [BB("ss")], [BB("ss")])
            P.op("dve", lambda e: e.reciprocal(ss[:, 3:4], ss[:, 2:3]), [BB("ss")], [BB("ss")])
            P.op("dve", lambda e: e.scalar_tensor_tensor(pot[i][:], x1[i][:], ss[:, 3:4], fnb[:], ALU.mult, ALU.mult),
                 [BB("x1", i), BB("ss"), BB("fnb"), BB("pot", i)], [BB("pot", i)])
            P.dma([("sp", K.out[r:r + 128, :], pot[i][:])], [BB("pot", i)], [BB("out", r)], BB("pot", i), final=True)

        for c in range(16):
            chunk(c)


def kernel(**inputs):
    inp = {k: np.asarray(v) for k, v in inputs.items()}
    nc = build_program()
    in_maps = [prep_core(inp, b, h) for b in range(4) for h in range(2)]
    res = run_bass_kernel_spmd(nc, in_maps, core_ids=list(range(8)))
    out = np.empty((4, NT, D), np.float32)
    for ci, r in enumerate(res.results):
        b, h = divmod(ci, 2)
        o = np.asarray(r["out"])
        if h == 0:
            out[b, :OWN] = o
        else:
            out[b, OWN:] = o[::-1]
    return out
```

```python
import contextlib
import numpy as np
import concourse.bass as bass
import concourse.mybir as mybir
from concourse.bass_utils import run_bass_kernel_spmd

F32 = mybir.dt.float32
BF16 = mybir.dt.bfloat16
AF = mybir.ActivationFunctionType
ALU = mybir.AluOpType
AX = mybir.AxisListType

D = 4096
NT = 4096
OWN = 2048
EPS = 1e-6
IN_W = 8256
NEG = -1.0e30


class Buf:
    __slots__ = ("name", "last_w", "readers")

    def __init__(self, name):
        self.name = name
        self.last_w = None
        self.readers = []


class _Grp:
    __slots__ = ("sem_key", "final", "ops")


class _Op:
    __slots__ = ("eng", "fn", "deps", "is_dma", "grp", "val", "needs_sig")


class Prog:
    ENGS = ("pe", "act", "dve", "pool", "sp")

    def __init__(self, nc):
        self.nc = nc
        self.ops = []
        self.dma_keys = {}
        self.final_groups = []
        self.bar_deps = set()
        self.bar_id = 0
        self.eng_bar = {e: 0 for e in self.ENGS}
        self.last_op = {e: None for e in self.ENGS}
        self.dma_since_bar = []

    def barrier(self):
        deps = set()
        for e in self.ENGS:
            if self.last_op[e] is not None:
                deps.add(self.last_op[e])
        for g in self.dma_since_bar:
            deps.add(g.ops[0])
        self.dma_since_bar = []
        self.bar_deps = deps
        self.bar_id += 1

    def _deps(self, eng, reads, writes):
        deps = set()
        for b in reads:
            if b.last_w is not None:
                deps.add(b.last_w)
        for b in writes:
            if b.last_w is not None:
                deps.add(b.last_w)
            deps.update(b.readers)
        if self.eng_bar[eng] != self.bar_id:
            deps.update(self.bar_deps)
            self.eng_bar[eng] = self.bar_id
        return deps

    def op(self, eng, fn, reads=(), writes=()):
        o = _Op()
        o.eng = eng
        o.fn = fn
        o.is_dma = False
        o.grp = None
        o.needs_sig = False
        o.deps = self._deps(eng, reads, writes)
        for b in reads:
            b.readers.append(o)
        for b in writes:
            b.last_w = o
            b.readers = []
        self.ops.append(o)
        self.last_op[eng] = o
        return o

    def dma(self, items, reads, writes, key, final=False, nobar=False, family=False):
        st = self.dma_keys.setdefault(key, {"count": 0, "last": None})
        g = _Grp()
        g.sem_key = key
        g.ops = []
        deps = set()
        for it in items:
            deps |= self._deps(it[0], reads, writes)
        if family:
            st["family"] = True
        elif st["last"] is not None:
            deps.update(st["last"].ops)
        for it in items:
            q, out, in_ = it[0], it[1], it[2]
            kw = it[3] if len(it) > 3 else {}
            o = _Op()
            o.eng = q
            o.is_dma = True
            o.grp = g
            o.needs_sig = True
            o.deps = deps
            o.fn = (lambda e, out=out, in_=in_, kw=kw: e.dma_start(out=out, in_=in_, **kw))
            g.ops.append(o)
            self.ops.append(o)
            st["count"] += 1
        g.final = 16 * st["count"]
        st["last"] = g
        for b in reads:
            b.readers.append(g.ops[0])
        for b in writes:
            b.last_w = g.ops[0]
            b.readers = []
        if not nobar:
            self.dma_since_bar.append(g)
        if final:
            self.final_groups.append(g)
        return g

    def emit(self, stack):
        nc = self.nc
        for o in self.ops:
            for d in o.deps:
                if not d.is_dma:
                    if d.eng == "pe" and o.eng == "pe" and not o.is_dma:
                        continue
                    d.needs_sig = True
        esem = {e: stack.enter_context(nc.semaphore("s_" + e)) for e in self.ENGS}
        dsem = {}
        for i, k in enumerate(self.dma_keys):
            dsem[k] = stack.enter_context(nc.semaphore("d%d" % i))
        cnt = {e: 0 for e in self.ENGS}
        for o in self.ops:
            if not o.is_dma and o.needs_sig:
                cnt[o.eng] += 1
                o.val = cnt[o.eng]
        per = {e: [o for o in self.ops if o.eng == e] for e in self.ENGS}
        finals = self.final_groups

        def run(eng_name, e):
            waited = {}
            for o in per[eng_name]:
                need = {}
                for d in o.deps:
                    if d.is_dma:
                        s, v = dsem[d.grp.sem_key], d.grp.final
                        if self.dma_keys[d.grp.sem_key].get("family"):
                            v = 16 * self.dma_keys[d.grp.sem_key]["count"]
                    else:
                        if d.eng == "pe" and eng_name == "pe" and not o.is_dma:
                            continue
                        s, v = esem[d.eng], d.val
                    if need.get(s, 0) < v:
                        need[s] = v
                for s, v in need.items():
                    if waited.get(s, 0) < v:
                        e.wait_ge(s, v)
                        waited[s] = v
                ins = o.fn(e)
                if o.is_dma:
                    ins.then_inc(dsem[o.grp.sem_key], 16)
                elif o.needs_sig:
                    ins.then_inc(esem[eng_name], 1)
            if eng_name == "sp":
                for g in finals:
                    v = g.final
                    if self.dma_keys[g.sem_key].get("family"):
                        v = 16 * self.dma_keys[g.sem_key]["count"]
                    e.wait_ge(dsem[g.sem_key], v)

        block = stack.enter_context(nc.Block())
        block.sync(lambda e: run("sp", e))
        block.scalar(lambda e: run("act", e))
        block.vector(lambda e: run("dve", e))
        block.gpsimd(lambda e: run("pool", e))
        block.tensor(lambda e: run("pe", e))


POOL_WINDOWS = (2, 4, 8, 16)
POOL_ND = (1, 1, 2, 4)


def _pool_consts(half):
    pm = np.zeros((128, 4, 9, 128), np.float32)
    invc = np.zeros((128, 16, 4), np.float32)
    pin = np.arange(128)
    for wi, w in enumerate(POOL_WINDOWS):
        lo, hi = (-(w // 2), w // 2 - 1) if half == 0 else (-(w // 2) + 1, w // 2)
        for d in range(-4, 5):
            dr = 2 * d + (pin[:, None] // 64) - (pin[None, :] // 64)
            dc = (pin[:, None] % 64) - (pin[None, :] % 64)
            pm[:, wi, d + 4, :] = ((dr >= lo) & (dr <= hi) & (dc >= lo) & (dc <= hi)).astype(np.float32)
        idx = np.arange(64)
        cnt = np.array([np.sum((idx >= r + lo) & (idx <= r + hi)) for r in range(64)], np.float32)
        t = np.arange(2048)
        ic = 1.0 / (cnt[t // 64] * cnt[t % 64])
        invc[:, :, wi] = ic.reshape(16, 128).T
    return pm, invc


def _tri_consts():
    k = np.arange(128)[:, None]
    i = np.arange(128)[None, :]
    c = np.zeros((128, 6, 128), np.float32)
    c[:, 0] = (k <= i)
    c[:, 1] = (k >= i)
    c[:, 2] = (k > i)
    c[:, 3] = (k < i)
    c[:, 4] = 1.0
    c[:, 5] = (k == i)
    return c


def prep_core(inp, b, half):
    f = np.float32
    flip = half == 1
    xl = inp["x"][b][::-1] if flip else inp["x"][b]
    cl = inp["ctx"][b][::-1] if flip else inp["ctx"][b]
    w_in = inp["w_in"][0]
    dtb = inp["dt_bias"][0]
    alog = inp["a_log"][0]
    cw = inp["conv_w"][0]
    if flip:
        w_in = np.concatenate([w_in[:, :8192], w_in[:, 8224:8256], w_in[:, 8192:8224]], axis=1)
        dtb = dtb[::-1]
        alog = alog[::-1]
        cw = cw[::-1]
    pm, invc = _pool_consts(half)
    d = {
        "xl": np.ascontiguousarray(xl, f),
        "ctxl": np.ascontiguousarray(cl, f),
        "cvec": np.ascontiguousarray(np.stack([inp["c"][b], inp["c_ctx"]]), f),
        "w_mod": inp["w_mod"][0],
        "b_mod": inp["b_mod"],
        "nrm": np.ascontiguousarray(np.stack([inp["norm1"][0], inp["norm2"][0], inp["final_norm"]]), f),
        "w_in": np.ascontiguousarray(w_in, f),
        "pool_w": inp["pool_w"][0],
        "pool_scale": inp["pool_scale"],
        "conv_w": np.ascontiguousarray(cw, f),
        "conv_b": inp["conv_b"],
        "dt_bias": np.ascontiguousarray(dtb.reshape(1, 64), f),
        "a_log": np.ascontiguousarray(alog.reshape(1, 64), f),
        "d_skip": inp["d_skip"],
        "ssd_norm": inp["ssd_norm"],
        "w_out": inp["w_out"][0],
        "peer_wq": inp["peer_wq"][0],
        "peer_keys": np.ascontiguousarray(inp["peer_keys"][0].reshape(16, 128, 128), f),
        "peer_u": inp["peer_u"][0],
        "peer_v": inp["peer_v"][0],
        "tri": _tri_consts(),
        "pm": pm,
        "invc": invc,
    }
    return d


INPUT_SHAPES = {
    "xl": [NT, D], "ctxl": [256, D], "cvec": [2, D], "w_mod": [D, 6 * D], "b_mod": [1, 6 * D],
    "nrm": [3, D], "w_in": [D, IN_W], "pool_w": [4, 512, 512], "pool_scale": [1, 2048],
    "conv_w": [5, D], "conv_b": [1, D], "dt_bias": [1, 64], "a_log": [1, 64], "d_skip": [1, 32],
    "ssd_norm": [1, 2048], "w_out": [D, D], "peer_wq": [D, 2048], "peer_keys": [16, 128, 128],
    "peer_u": [16384, D], "peer_v": [16384, D], "tri": [128, 6, 128], "pm": [128, 4, 9, 128],
    "invc": [128, 16, 4],
}


class Ctx:
    pass


def build_program(stop=99, dbg=()):
    nc = bass.Bass("TRN2", target_bir_lowering=False)
    I = {k: nc.dram_tensor(k, shp, F32, kind="ExternalInput").ap() for k, shp in INPUT_SHAPES.items()}
    out = nc.dram_tensor("out", [OWN, D], F32, kind="ExternalOutput").ap()

    def scratch(name, shape, dt):
        kind = "ExternalOutput" if name in dbg else "Internal"
        return nc.dram_tensor(name, shape, dt, kind=kind).ap()

    S = {}
    S["modrow"] = scratch("modrow", [2, 6 * D], F32)
    S["w_inB"] = scratch("w_inB", [16, 128, 32 * 512], BF16)
    S["w_dtB"] = scratch("w_dtB", [128, 32 * 64], BF16)
    S["w_outB"] = scratch("w_outB", [2, 4, 128, 8 * 2048], BF16)
    S["wqB"] = scratch("wqB", [8, 128, 32 * 256], BF16)
    S["poolwB"] = scratch("poolwB", [4, 512, 512], BF16)
    S["VB"] = scratch("VB", [2, 16, 128, 8 * 2048], BF16)
    S["UT"] = scratch("UT", [32, 128, 32 * 512], BF16)
    S["XH"] = scratch("XH", [5120, 2048], BF16)
    S["BMK"] = scratch("BMK", [5120, 1024], BF16)
    S["BMT"] = scratch("BMT", [8, 128, 5120], BF16)
    S["CMT"] = scratch("CMT", [8, 128, 5120], BF16)
    S["DT"] = scratch("DT", [NT + 256, 64], F32)
    S["SZ"] = scratch("SZ", [OWN, 2048], BF16)
    S["VV"] = scratch("VV", [2560, 2048], BF16)
    S["HFS"] = scratch("HFS", [16, 128, 2048], BF16)
    S["MIXT"] = scratch("MIXT", [32, 128, OWN], BF16)
    S["X1"] = scratch("X1", [OWN, D], F32)
    S["H2T"] = scratch("H2T", [32, 128, OWN], BF16)
    S["SS"] = scratch("SS", [OWN, 2048], F32)
    S["TN"] = scratch("TN", [OWN, 16], F32)
    S["PO"] = scratch("PO", [OWN, D], F32)
    S["HCTX"] = scratch("HCTX", [2, 128, 2048], F32)

    bufs = {}

    def BB(*key):
        b = bufs.get(key)
        if b is None:
            b = bufs[key] = Buf(str(key))
        return b

    with contextlib.ExitStack() as st:
        P = Prog(nc)
        K = Ctx()
        K.nc, K.P, K.I, K.S, K.BB, K.out = nc, P, I, S, BB, out

        def psb(name, shape, dt):
            return st.enter_context(nc.sbuf_tensor("g_" + name, shape, dt))

        K.tri = psb("tri", [128, 6, 128], F32)
        K.ident = K.tri[:, 5, :]
        K.colv = psb("colv", [128, 8, 32], F32)
        K.gam = psb("gam", [128, 3, 32], F32)
        P.dma([("sp", K.tri[:], I["tri"])], [], [BB("tri")], BB("tri"))

        def stage(dst, src, rows, step, name):
            for r0 in range(0, rows, step):
                P.dma([("pool", dst[r0:r0 + step], src[r0:r0 + step])], [], [BB(name, r0)], BB(name, "k"))
        for cb in range(16):
            P.dma([("pool", S["w_inB"][cb].rearrange("p (kc n) -> p kc n", n=512),
                    I["w_in"][:, cb * 512:(cb + 1) * 512].rearrange("(kc p) n -> p kc n", p=128))],
                  [], [BB("w_inB", cb)], BB("w_inB", "k"), family=True)
        P.dma([("pool", S["w_dtB"].rearrange("p (kc n) -> p kc n", n=64),
                I["w_in"][:, 8192:8256].rearrange("(kc p) n -> p kc n", p=128))], [], [BB("w_dtB")], BB("w_dtB", "k"))
        stage(S["poolwB"], I["pool_w"], 4, 4, "poolwB")

        def stage_late():
            for dmh in range(2):
                for fcg in range(4):
                    P.dma([("pool", S["w_outB"][dmh, fcg].rearrange("p (a n) -> p a n", n=2048),
                            I["w_out"][fcg * 1024:(fcg + 1) * 1024, dmh * 2048:(dmh + 1) * 2048].rearrange("(a p) n -> p a n", p=128))],
                          [], [BB("w_outB", dmh, fcg)], BB("w_outB", "k"), nobar=True, family=True)
            for cbk in range(8):
                P.dma([("pool", S["wqB"][cbk].rearrange("p (kc n) -> p kc n", n=256),
                        I["peer_wq"][:, cbk * 256:(cbk + 1) * 256].rearrange("(kc p) n -> p kc n", p=128))],
                      [], [BB("wqB", cbk)], BB("wqB", "k"), nobar=True, family=True)
        def stage_late2():
            for dmh in range(2):
                for ecg in range(16):
                    P.dma([("pool", S["VB"][dmh, ecg].rearrange("p (a n) -> p a n", n=2048),
                            I["peer_v"][ecg * 1024:(ecg + 1) * 1024, dmh * 2048:(dmh + 1) * 2048].rearrange("(a p) n -> p a n", p=128))],
                          [], [BB("VB", dmh, ecg)], BB("VB", "k"), nobar=True, family=True)
        K.stage_late = stage_late
        K.stage_late2 = stage_late2

        phase_mod(K)
        P.barrier()
        if stop >= 1:
            phase_inproj(K)
            P.barrier()
        if stop >= 2:
            phase_ssd(K)
            P.barrier()
        if stop >= 3:
            phase_pool(K)
            P.barrier()
        if stop >= 4:
            phase_wout(K)
            P.barrier()
        if stop >= 5:
            phase_peer_prep(K)
            P.barrier()
        if stop >= 6:
            phase_peer_dense(K)
            P.barrier()
        if stop >= 7:
            phase_final(K)
        else:
            with contextlib.ExitStack() as ph:
                t = ph.enter_context(nc.sbuf_tensor("g_dummy_o", [128, 64], F32))
                P.op("dve", lambda e: e.memset(t[:], 0.0), [], [BB("dummy_o")])
                P.dma([("sp", out[0:128, 0:64], t[:])], [BB("dummy_o")], [BB("out", "d")], BB("dummy_o"), final=True)
        for g in list(P.dma_since_bar):
            if g not in P.final_groups:
                P.final_groups.append(g)
        P.emit(st)
    return nc


def split_dma(P, out, in_, reads, writes, key, n=2, axis=1, queues=("sp", "act"), kw=None):
    size = out.shape[axis]
    step = size // n
    items = []
    for i in range(n):
        sl = [slice(None)] * len(out.shape)
        sl[axis] = slice(i * step, (i + 1) * step)
        sl = tuple(sl)
        it = (queues[i % len(queues)], out[sl], in_[sl])
        if kw:
            it = it + (kw,)
        items.append(it)
    return P.dma(items, reads, writes, key)


def phase_mod(K):
    nc, P, I, S, BB = K.nc, K.P, K.I, K.S, K.BB
    slow = {"allow_slow_non_contiguous": True}
    with contextlib.ExitStack() as ph:
        sb = lambda n, s, dt: ph.enter_context(nc.sbuf_tensor("m_" + n, s, dt))
        cT = sb("cT", [128, 2, 32], F32)
        scT = sb("scT", [128, 32, 2], F32)
        wm = [sb("wm%d" % i, [128, 32, 512], F32) for i in range(2)]
        bm2 = [sb("bm2_%d" % i, [2, 512], F32) for i in range(2)]
        res = [sb("res%d" % i, [2, 512], F32) for i in range(2)]
        pm_ = [ph.enter_context(nc.psum_tensor("m_pm%d" % i, [128, 512], F32)) for i in range(2)]
        for r in range(2):
            P.dma([("sp", cT[:, r, :], I["cvec"][r, :].rearrange("(kc p) -> p kc", p=128), slow)],
                  [], [BB("cT", r)], BB("cT", r))
            P.op("act", lambda e, r=r: e.activation(scT[:, :, r], cT[:, r, :], AF.Silu), [BB("cT", r)], [BB("scT")])
        for cb in range(48):
            i = cb % 2
            cols = slice(cb * 512, (cb + 1) * 512)
            split_dma(P, wm[i][:], I["w_mod"][:, cols].rearrange("(kc p) n -> p kc n", p=128),
                      [], [BB("wm", i)], BB("wm", i), n=4, axis=1)
            P.dma([("sp", bm2[i][:], I["b_mod"][0:1, cols].partition_broadcast(2))], [], [BB("bm2", i)], BB("bm2", i))
            for kc in range(32):
                P.op("pe", lambda e, i=i, kc=kc: e.matmul(pm_[i][0:2, :], scT[:, kc, :], wm[i][:, kc, :],
                                                           start=(kc == 0), stop=(kc == 31)),
                     [BB("scT"), BB("wm", i)], [BB("pm", i)])
            P.op("dve", lambda e, i=i: e.tensor_tensor(res[i][:], pm_[i][0:2, :], bm2[i][:], ALU.add),
                 [BB("pm", i), BB("bm2", i)], [BB("res", i)])
            P.dma([("sp", S["modrow"][:, cols], res[i][:])], [BB("res", i)], [BB("modrow")], BB("res", i))
        srcs = [S["modrow"][0, 0:D], S["modrow"][0, D:2 * D], S["modrow"][0, 3 * D:4 * D], S["modrow"][0, 4 * D:5 * D],
                S["modrow"][1, 0:D], S["modrow"][1, D:2 * D], I["nrm"][0, :], I["nrm"][1, :]]
        for v, src in enumerate(srcs):
            P.dma([("sp" if v % 2 == 0 else "act", K.colv[:, v, :], src.rearrange("(kc p) -> p kc", p=128), slow)],
                  [BB("modrow")], [BB("colv", v)], BB("colv", v))
        for gi, (sci, ni) in enumerate([(1, 6), (3, 7), (5, 6)]):
            P.op("dve", lambda e, gi=gi, sci=sci, ni=ni: e.scalar_tensor_tensor(
                K.gam[:, gi, :], K.colv[:, sci, :], 1.0, K.colv[:, ni, :], ALU.add, ALU.mult),
                [BB("colv", sci), BB("colv", ni)], [BB("gam")])


def phase_inproj(K):
    nc, P, I, S, BB = K.nc, K.P, K.I, K.S, K.BB
    K.stage_late()
    slow = {"allow_slow_non_contiguous": True}
    with contextlib.ExitStack() as ph:
        sb = lambda n, s, dt: ph.enter_context(nc.sbuf_tensor("a_" + n, s, dt))
        xt = sb("xt", [128, D], F32)
        xs = sb("xs", [128, D], F32)
        ss = sb("ss", [128, 4], F32)
        hT = sb("hT", [128, 32, 512], BF16)
        wt = [sb("wt%d" % i, [128, 32, 512], BF16) for i in range(2)]
        wdt = sb("wdt", [128, 32, 64], BF16)
        pcv = [sb("pcv%d" % i, [128, 516], F32) for i in range(2)]
        a32 = [sb("a32_%d" % i, [128, 512], F32) for i in range(2)]
        xc32 = [sb("xc32_%d" % i, [128, 512], F32) for i in range(2)]
        xcb = [sb("xcb%d" % i, [128, 512], BF16) for i in range(2)]
        carry = sb("carry", [128, 32, 4], F32)
        cw = sb("cw", [128, 32, 5], F32)
        cbs = sb("cbs", [128, 32], F32)
        xh_tok = sb("xh_tok", [128, 4, 3072], BF16)
        vz_tok = sb("vz_tok", [128, 4, 2048], BF16)
        dtb = sb("dtb", [128, 64], F32)
        d1 = sb("d1", [128, 64], F32)
        d2 = sb("d2", [128, 64], F32)
        tp = [ph.enter_context(nc.psum_tensor("a_tp%d" % i, [128, 512], F32)) for i in range(2)]
        acc = [ph.enter_context(nc.psum_tensor("a_acc%d" % i, [128, 512], F32)) for i in range(4)]
        ptk = [ph.enter_context(nc.psum_tensor("a_ptk%d" % i, [128, 512], F32)) for i in range(2)]
        print("inproj sbuf remaining", nc.sbuf_bytes_remaining)

        P.dma([("sp", cw[:, :, k], I["conv_w"][k, :].rearrange("(cc p) -> p cc", p=128), slow) for k in range(5)],
              [], [BB("cw")], BB("cw"))
        P.dma([("act", cbs[:], I["conv_b"][0, :].rearrange("(cc p) -> p cc", p=128), slow)], [], [BB("cbs")], BB("cbs"))
        P.dma([("sp", dtb[:], I["dt_bias"][0:1, :].partition_broadcast(128))], [], [BB("dtb")], BB("dtb"))
        P.dma([("sp", wdt[:], S["w_dtB"].rearrange("p (kc n) -> p kc n", n=64))], [BB("w_dtB")], [BB("wdt")], BB("wdt"))
        cnt = {"wt": 0, "acc": 0, "tp": 0, "ptk": 0, "cc": 0}

        def load_w(col0):
            i = cnt["wt"] % 2
            cnt["wt"] += 1
            cb = col0 // 512
            P.dma([("sp", wt[i][:], S["w_inB"][cb].rearrange("p (kc n) -> p kc n", n=512))],
                  [BB("w_inB", cb)], [BB("wt", i)], BB("wt", i))
            return i

        def run_seq(src, ntok, BT, gi, bi, row0, dtrow0, flags):
            nq = BT // 128
            nblk = ntok // BT
            P.op("dve", lambda e: e.memset(carry[:], 0.0), [], [BB("carry")])
            for j in range(nblk + 1):
                flush = j == nblk
                fl = flags(j) if not flush else flags(nblk - 1)
                if not flush:
                    for tq in range(nq):
                        r = j * BT + tq * 128
                        P.dma([("sp", xt[:], src[r:r + 128, :])], [], [BB("xt")], BB("xt"))
                        P.op("dve", lambda e: e.memset(ss[:], 0.0), [], [BB("ss")])
                        P.op("act", lambda e: e.activation(xs[:], xt[:], AF.Square, accum_out=ss[:, 0:1]),
                             [BB("xt"), BB("ss")], [BB("xs"), BB("ss")])
                        P.op("dve", lambda e: e.tensor_scalar(ss[:, 1:2], ss[:, 0:1], 1.0 / D, EPS, ALU.mult, ALU.add),
                             [BB("ss")], [BB("ss")])
                        P.op("act", lambda e: e.activation(ss[:, 2:3], ss[:, 1:2], AF.Sqrt), [BB("ss")], [BB("ss")])
                        P.op("dve", lambda e: e.reciprocal(ss[:, 3:4], ss[:, 2:3]), [BB("ss")], [BB("ss")])
                        P.op("dve", lambda e: e.tensor_scalar(xs[:], xt[:], ss[:, 3:4], None, ALU.mult),
                             [BB("xt"), BB("ss")], [BB("xs")])
                        for kc in range(32):
                            if kc % 4 == 0:
                                ti = cnt["tp"] % 2
                                cnt["tp"] += 1
                            P.op("pe", lambda e, ti=ti, kc=kc: e.transpose(
                                tp[ti][:, (kc % 4) * 128:(kc % 4 + 1) * 128], xs[:, kc * 128:(kc + 1) * 128], K.ident),
                                [BB("xs"), BB("tri")], [BB("tp", ti)])
                            if kc % 4 == 3:
                                for k2 in range(kc - 3, kc + 1):
                                    P.op("act", lambda e, ti=ti, k2=k2, tq=tq: e.activation(
                                        hT[:, k2, tq * 128:(tq + 1) * 128], tp[ti][:, (k2 % 4) * 128:(k2 % 4 + 1) * 128],
                                        AF.Identity, bias=K.colv[:, bi, k2:k2 + 1], scale=K.gam[:, gi, k2:k2 + 1]),
                                        [BB("tp", ti), BB("gam"), BB("colv", bi)], [BB("hT", tq)])
                hT_bufs = [BB("hT", tq) for tq in range(nq)]
                ncc = 32 if fl["cm"] else 24
                for cbk in range(ncc // 4):
                    if not flush:
                        wi = load_w(4096 + cbk * 512)
                    for sub in range(4):
                        cc = cbk * 4 + sub
                        ci = cnt["cc"] % 2
                        cnt["cc"] += 1
                        if not flush:
                            ai = cnt["acc"] % 4
                            cnt["acc"] += 1
                            for kc in range(32):
                                P.op("pe", lambda e, ai=ai, wi=wi, kc=kc, sub=sub: e.matmul(
                                    acc[ai][:, 0:BT], wt[wi][:, kc, sub * 128:(sub + 1) * 128], hT[:, kc, 0:BT],
                                    start=(kc == 0), stop=(kc == 31)),
                                    [BB("wt", wi)] + hT_bufs, [BB("acc", ai)])
                            P.op("act", lambda e, ai=ai, ci=ci: e.copy(pcv[ci][:, 4:4 + BT], acc[ai][:, 0:BT]),
                                 [BB("acc", ai)], [BB("pcv", ci)])
                        else:
                            P.op("pool", lambda e, ci=ci: e.memset(pcv[ci][:, 4:4 + BT], 0.0), [], [BB("pcv", ci)])
                        P.op("pool", lambda e, ci=ci, cc=cc: e.tensor_copy(pcv[ci][:, 0:4], carry[:, cc, :]),
                             [BB("carry")], [BB("pcv", ci)])
                        P.op("pool", lambda e, ci=ci, cc=cc: e.tensor_copy(carry[:, cc, :], pcv[ci][:, BT:BT + 4]),
                             [BB("pcv", ci)], [BB("carry")])
                        P.op("dve", lambda e, ci=ci, cc=cc: e.tensor_scalar(
                            a32[ci][:, 0:BT], pcv[ci][:, 0:BT], cw[:, cc, 0:1], None, ALU.mult),
                            [BB("pcv", ci), BB("cw")], [BB("a32", ci)])
                        for k in range(1, 5):
                            P.op("dve", lambda e, ci=ci, cc=cc, k=k: e.scalar_tensor_tensor(
                                a32[ci][:, 0:BT], pcv[ci][:, k:k + BT], cw[:, cc, k:k + 1], a32[ci][:, 0:BT],
                                ALU.mult, ALU.add), [BB("pcv", ci), BB("cw"), BB("a32", ci)], [BB("a32", ci)])
                        P.op("act", lambda e, ci=ci, cc=cc: e.activation(
                            xc32[ci][:, 0:BT], a32[ci][:, 0:BT], AF.Silu, bias=cbs[:, cc:cc + 1]),
                            [BB("a32", ci), BB("cbs")], [BB("xc32", ci)])
                        if cc < 24:
                            pi = cnt["ptk"] % 2
                            cnt["ptk"] += 1
                            for tq in range(nq):
                                P.op("pe", lambda e, pi=pi, ci=ci, tq=tq: e.transpose(
                                    ptk[pi][:, tq * 128:(tq + 1) * 128], xc32[ci][:, tq * 128:(tq + 1) * 128], K.ident),
                                    [BB("xc32", ci), BB("tri")], [BB("ptk", pi)])
                            P.op("dve", lambda e, pi=pi, cc=cc: e.tensor_copy(
                                xh_tok[:, 0:nq, cc * 128:(cc + 1) * 128],
                                ptk[pi][:, 0:BT].rearrange("p (q c) -> p q c", c=128)),
                                [BB("ptk", pi)], [BB("xh_tok")])
                        if cc >= 16:
                            P.op("act", lambda e, ci=ci: e.copy(xcb[ci][:, 0:BT], xc32[ci][:, 0:BT]),
                                 [BB("xc32", ci)], [BB("xcb", ci)])
                            dst = S["BMT"][cc - 16] if cc < 24 else S["CMT"][cc - 24]
                            P.dma([("act", dst[:, row0 + j * BT: row0 + (j + 1) * BT], xcb[ci][:, 0:BT])],
                                  [BB("xcb", ci)], [BB("FT", cc, row0, j)], BB("xcb", ci))
                rr = row0 + j * BT
                P.dma([("sp", S["XH"][rr:rr + BT, :].rearrange("(q p) f -> p q f", p=128), xh_tok[:, 0:nq, 0:2048]),
                       ("sp", S["BMK"][rr:rr + BT, :].rearrange("(q p) f -> p q f", p=128), xh_tok[:, 0:nq, 2048:3072])],
                      [BB("xh_tok")], [BB("XHBMK", row0, j)], BB("xh_tok"))
                if flush:
                    continue
                for kind in ("v", "z"):
                    if not fl[kind]:
                        continue
                    cb0 = 0 if kind == "v" else 4
                    for cbk in range(4):
                        wi = load_w((cb0 + cbk) * 512)
                        for tq in range(nq):
                            ai = cnt["acc"] % 4
                            cnt["acc"] += 1
                            for kc in range(32):
                                P.op("pe", lambda e, ai=ai, wi=wi, kc=kc, tq=tq: e.matmul(
                                    acc[ai][:, :], hT[:, kc, tq * 128:(tq + 1) * 128], wt[wi][:, kc, :],
                                    start=(kc == 0), stop=(kc == 31)),
                                    [BB("wt", wi), BB("hT", tq)], [BB("acc", ai)])
                            if kind == "v":
                                P.op("act", lambda e, ai=ai, tq=tq, cbk=cbk: e.copy(
                                    vz_tok[:, tq, cbk * 512:(cbk + 1) * 512], acc[ai][:, :]),
                                    [BB("acc", ai)], [BB("vz_tok")])
                            else:
                                P.op("act", lambda e, ai=ai, tq=tq, cbk=cbk: e.activation(
                                    vz_tok[:, tq, cbk * 512:(cbk + 1) * 512], acc[ai][:, :], AF.Silu),
                                    [BB("acc", ai)], [BB("vz_tok")])
                    dst = S["VV"] if kind == "v" else S["SZ"]
                    P.dma([("act", dst[j * BT:(j + 1) * BT, :].rearrange("(q p) f -> p q f", p=128), vz_tok[:, 0:nq, :])],
                          [BB("vz_tok")], [BB(kind, j)], BB("vz_tok"))
                for tq in range(nq):
                    ai = cnt["acc"] % 4
                    cnt["acc"] += 1
                    for kc in range(32):
                        P.op("pe", lambda e, ai=ai, kc=kc, tq=tq: e.matmul(
                            acc[ai][:, 0:64], hT[:, kc, tq * 128:(tq + 1) * 128], wdt[:, kc, :],
                            start=(kc == 0), stop=(kc == 31)), [BB("wdt"), BB("hT", tq)], [BB("acc", ai)])
                    P.op("dve", lambda e, ai=ai: e.tensor_tensor(d1[:], acc[ai][:, 0:64], dtb[:], ALU.add),
                         [BB("acc", ai), BB("dtb")], [BB("d1")])
                    P.op("act", lambda e: e.activation(d2[:], d1[:], AF.Exp), [BB("d1")], [BB("d2")])
                    P.op("act", lambda e: e.activation(d1[:], d2[:], AF.Ln, bias=1.0), [BB("d2"), BB("d1")], [BB("d1")])
                    r = dtrow0 + j * BT + tq * 128
                    P.dma([("act", S["DT"][r:r + 128, :], d1[:])], [BB("d1")], [BB("DT", r)], BB("d1"))

        run_seq(I["ctxl"], 256, 256, 2, 4, 4608, NT, lambda j: {"cm": False, "v": False, "z": False})
        run_seq(I["xl"], NT, 512, 0, 0, 0, 0,
                lambda j: {"cm": j < 5, "v": j < 5, "z": j < 4})


def phase_ssd(K):
    nc, P, I, S, BB = K.nc, K.P, K.I, K.S, K.BB
    tri = K.tri
    with contextlib.ExitStack() as ph:
        sb = lambda n, s, dt: ph.enter_context(nc.sbuf_tensor("s_" + n, s, dt))
        L = []
        for i in range(2):
            L.append({"xh": sb("xh%d" % i, [128, 2048], BF16), "bm": sb("bm%d" % i, [128, 1024], BF16),
                      "dt": sb("dt%d" % i, [128, 64], F32), "bmT": sb("bmT%d" % i, [128, 8, 128], BF16),
                      "cmT": sb("cmT%d" % i, [128, 8, 128], BF16), "sz": sb("sz%d" % i, [128, 2048], BF16),
                      "hfb": sb("hfb%d" % i, [128, 2048], BF16)})
        rhsD = [sb("rhsD%d" % d, [128, 32, 128], F32) for d in range(2)]
        MT = [sb("MT%d" % d, [128, 32, 128], BF16) for d in range(2)]
        E32 = [sb("E32_%d" % i, [128, 512], F32) for i in range(2)]
        CBm = [sb("CBm%d" % d, [128, 8, 128], F32) for d in range(2)]
        xdt = [sb("xdt%d" % d, [128, 2048], BF16) for d in range(2)]
        xw = sb("xw", [128, 2048], BF16)
        yacc = sb("yacc", [128, 2048], F32)
        tmp = sb("tmp", [128, 2048], F32)
        H = [sb("H%d" % d, [128, 2048], F32) for d in range(2)]
        Hb = [sb("Hb%d" % d, [128, 2048], BF16) for d in range(2)]
        ssdn = sb("ssdn", [128, 2048], F32)
        mixsb = sb("mixsb", [128, 16, 128], BF16)
        a_bc = sb("a_bc", [128, 64], F32)
        dsk = sb("dsk", [128, 32], F32)
        dA = sb("dA", [128, 64], F32)
        css = sb("css", [128, 128], F32)
        ea = sb("ea", [128, 64], F32)
        t64 = sb("t64", [128, 64], F32)
        wgt = sb("wgt", [128, 64], F32)
        cd = sb("cd", [128, 64], F32)
        ss = sb("ss", [128, 4], F32)
        py = ph.enter_context(nc.psum_tensor("s_py", [128, 2048], F32))
        pz = ph.enter_context(nc.psum_tensor("s_pz", [128, 2048], F32))
        print("ssd sbuf remaining", nc.sbuf_bytes_remaining)
        PZ = [BB("pz", i) for i in range(4)]

        P.dma([("sp", ssdn[:], I["ssd_norm"][0:1, :].partition_broadcast(128))], [], [BB("ssdn")], BB("ssdn"))
        P.dma([("sp", dsk[:], I["d_skip"][0:1, :].partition_broadcast(128))], [], [BB("dsk")], BB("dsk"))
        P.dma([("sp", a_bc[:], I["a_log"][0:1, :].partition_broadcast(128))], [], [BB("a_bc")], BB("a_bc"))
        P.op("act", lambda e: e.activation(a_bc[:], a_bc[:], AF.Exp), [BB("a_bc")], [BB("a_bc")])
        P.op("dve", lambda e: e.tensor_scalar(a_bc[:], a_bc[:], -1.0, None, ALU.mult), [BB("a_bc")], [BB("a_bc")])
        for d in range(2):
            P.op("dve", lambda e, d=d: e.memset(H[d][:], 0.0), [], [BB("H", d)])
        cnt = {"l": 0, "e": 0}

        def bc_hp(ap32):
            return ap32.unsqueeze(2).to_broadcast([128, 32, 64])

        def v_hp(ap):
            return ap.rearrange("p (h q) -> p h q", q=64)

        def load(c, base, dtbase, full):
            i = cnt["l"] % 2
            cnt["l"] += 1
            T = L[i]
            r = base + 128 * c + 2
            items = [("sp", T["xh"][:], S["XH"][r:r + 128, :]), ("act", T["bm"][:], S["BMK"][r:r + 128, :]),
                     ("sp", T["dt"][:], S["DT"][dtbase + 128 * c: dtbase + 128 * c + 128, :])]
            wr = [BB("L", i, "xh"), BB("L", i, "bm"), BB("L", i, "dt")]
            if full:
                items += [("act", T["bmT"][:], S["BMT"][:, :, r:r + 128].rearrange("g n t -> n g t")),
                          ("sp", T["cmT"][:], S["CMT"][:, :, r:r + 128].rearrange("g n t -> n g t")),
                          ("act", T["sz"][:], S["SZ"][128 * c:128 * c + 128, :]),
                          ("sp", T["hfb"][:], S["HFS"][c])]
                wr += [BB("L", i, "bmT"), BB("L", i, "cmT"), BB("L", i, "sz"), BB("L", i, "hfb")]
            P.dma(items, [BB("HFS", c)] if full else [], wr, BB("L", i, "k"))
            return i

        def derive(i):
            T = L[i]
            P.op("dve", lambda e: e.tensor_tensor(dA[:], T["dt"][:], a_bc[:], ALU.mult),
                 [BB("L", i, "dt"), BB("a_bc")], [BB("dA")])
            P.op("pe", lambda e: e.matmul(pz[:, 0:32], tri[:, 0, :], dA[:, 0:32], start=True, stop=True),
                 [BB("dA"), BB("tri")], [PZ[0]])
            P.op("pe", lambda e: e.matmul(pz[:, 32:64], tri[:, 1, :], dA[:, 32:64], start=True, stop=True),
                 [BB("dA"), BB("tri")], [PZ[0]])
            P.op("pe", lambda e: e.matmul(pz[:, 64:128], tri[:, 4, :], dA[:, 0:64], start=True, stop=True),
                 [BB("dA"), BB("tri")], [PZ[0]])
            P.op("dve", lambda e: e.tensor_copy(css[:], pz[:, 0:128]), [PZ[0]], [BB("css")])
            P.op("act", lambda e: e.activation(ea[:], css[:, 0:64], AF.Exp), [BB("css")], [BB("ea")])
            P.op("dve", lambda e: e.tensor_tensor(t64[:], css[:, 64:128], css[:, 0:64], ALU.subtract),
                 [BB("css")], [BB("t64")])
            P.op("act", lambda e: e.activation(t64[:], t64[:], AF.Exp), [BB("t64")], [BB("t64")])
            P.op("dve", lambda e: e.tensor_tensor(wgt[:], t64[:], T["dt"][:], ALU.mult),
                 [BB("t64"), BB("L", i, "dt")], [BB("wgt")])
            P.op("act", lambda e: e.activation(cd[:], css[:, 64:128], AF.Exp), [BB("css")], [BB("cd")])

        def update(i, d):
            T = L[i]
            P.op("pool", lambda e: e.tensor_tensor(v_hp(xw[:]), v_hp(T["xh"][:]), bc_hp(wgt[:, 32 * d:32 * d + 32]), ALU.mult),
                 [BB("L", i, "xh"), BB("wgt")], [BB("xw")])
            for g in range(8):
                P.op("pe", lambda e, g=g: e.matmul(pz[:, g * 256:(g + 1) * 256], T["bm"][:, g * 128:(g + 1) * 128],
                                                   xw[:, g * 256:(g + 1) * 256], start=True, stop=True),
                     [BB("L", i, "bm"), BB("xw")], [PZ[g // 2]])
            P.op("dve", lambda e: e.tensor_tensor(v_hp(H[d][:]), v_hp(H[d][:]), bc_hp(cd[:, 32 * d:32 * d + 32]), ALU.mult),
                 [BB("H", d), BB("cd")], [BB("H", d)])
            P.op("dve", lambda e: e.tensor_tensor(H[d][:], H[d][:], pz[:, :], ALU.add), [BB("H", d)] + PZ, [BB("H", d)])
            P.op("act", lambda e: e.copy(Hb[d][:], H[d][:]), [BB("H", d)], [BB("Hb", d)])

        for c in (0, 1):
            i = load(c, 4608, NT, False)
            derive(i)
            update(i, 0)
        for c in (1, 0):
            i = load(c, 4608, NT, False)
            derive(i)
            update(i, 1)
        for d in range(2):
            P.dma([("sp", S["HCTX"][d], H[d][:])], [BB("H", d)], [BB("HCTX", d)], BB("H", d))
        P.dma([("sp", S["HFS"][0], Hb[0][:])], [BB("Hb", 0)], [BB("HFS", 0)], BB("Hb", 0))
        for c in range(15):
            i = load(c, 0, 0, False)
            derive(i)
            update(i, 0)
            P.dma([("sp", S["HFS"][c + 1], Hb[0][:])], [BB("Hb", 0)], [BB("HFS", c + 1)], BB("Hb", 0))
        for c in range(31, 15, -1):
            i = load(c, 0, 0, False)
            derive(i)
            update(i, 1)
        def full_chunk(c):
            i = load(c, 0, 0, True)
            T = L[i]
            derive(i)
            for g in range(8):
                P.op("pe", lambda e, g=g: e.matmul(pz[:, 512 + g * 128:512 + (g + 1) * 128], T["bmT"][:, g, :], T["cmT"][:, g, :],
                                                   start=True, stop=True),
                     [BB("L", i, "bmT"), BB("L", i, "cmT")], [PZ[1 + g // 4]])
            for d in range(2):
                P.op("dve", lambda e, d=d: e.tensor_tensor(
                    CBm[d][:], pz[:, 512:1536].rearrange("p (g t) -> p g t", t=128),
                    tri[:, d, :].unsqueeze(1).to_broadcast([128, 8, 128]), ALU.mult),
                    [PZ[1], PZ[2], BB("tri")], [BB("CBm", d)])
                P.op("pool", lambda e, d=d: e.tensor_tensor(
                    rhsD[d][:], tri[:, d, :].unsqueeze(1).to_broadcast([128, 32, 128]),
                    dA[:, 32 * d:32 * d + 32].unsqueeze(2).to_broadcast([128, 32, 128]), ALU.mult),
                    [BB("dA"), BB("tri")], [BB("rhsD", d)])
                P.op("pool", lambda e, d=d: e.tensor_tensor(
                    v_hp(xdt[d][:]), v_hp(T["xh"][:]), bc_hp(T["dt"][:, 32 * d:32 * d + 32]), ALU.mult),
                    [BB("L", i, "xh"), BB("L", i, "dt")], [BB("xdt", d)])
            for d in range(2):
                for g in range(8):
                    ei = cnt["e"] % 2
                    cnt["e"] += 1
                    bank = 0 if ei == 0 else 3
                    P.op("pe", lambda e, d=d, g=g, bank=bank: e.matmul(
                        pz[:, bank * 512:(bank + 1) * 512], tri[:, 2 + d, :],
                        rhsD[d][:, 4 * g:4 * g + 4, :], start=True, stop=True),
                        [BB("rhsD", d), BB("tri")], [PZ[bank]])
                    P.op("act", lambda e, ei=ei, bank=bank: e.activation(E32[ei][:], pz[:, bank * 512:(bank + 1) * 512], AF.Exp),
                         [PZ[bank]], [BB("E32", ei)])
                    P.op("dve", lambda e, d=d, g=g, ei=ei: e.tensor_tensor(
                        MT[d][:, 4 * g:4 * g + 4, :], E32[ei][:].rearrange("p (r t) -> p r t", t=128),
                        CBm[d][:, g, :].unsqueeze(1).to_broadcast([128, 4, 128]), ALU.mult),
                        [BB("E32", ei), BB("CBm", d)], [BB("MT", d)])
            for h in range(32):
                P.op("pe", lambda e, h=h: e.matmul(py[:, h * 64:(h + 1) * 64], MT[0][:, h, :], xdt[0][:, h * 64:(h + 1) * 64],
                                                   start=True, stop=False), [BB("MT", 0), BB("xdt", 0)], [BB("py")])
                P.op("pe", lambda e, h=h: e.matmul(py[:, h * 64:(h + 1) * 64], MT[1][:, h, :], xdt[1][:, h * 64:(h + 1) * 64],
                                                   start=False, stop=True), [BB("MT", 1), BB("xdt", 1)], [BB("py")])
            for g in range(8):
                P.op("pe", lambda e, g=g: e.matmul(pz[:, g * 256:(g + 1) * 256], T["cmT"][:, g, :], T["hfb"][:, g * 256:(g + 1) * 256],
                                                   start=True, stop=True), [BB("L", i, "cmT"), BB("L", i, "hfb")], [PZ[g // 2]])
            P.op("dve", lambda e: e.tensor_tensor(v_hp(yacc[:]), v_hp(pz[:, :]), bc_hp(ea[:, 0:32]), ALU.mult),
                 PZ + [BB("ea")], [BB("yacc")])
            P.op("dve", lambda e: e.tensor_tensor(yacc[:], yacc[:], py[:, :], ALU.add), [BB("yacc"), BB("py")], [BB("yacc")])
            for g in range(8):
                P.op("pe", lambda e, g=g: e.matmul(pz[:, g * 256:(g + 1) * 256], T["cmT"][:, g, :], Hb[1][:, g * 256:(g + 1) * 256],
                                                   start=True, stop=True), [BB("L", i, "cmT"), BB("Hb", 1)], [PZ[g // 2]])
            P.op("dve", lambda e: e.tensor_tensor(v_hp(tmp[:]), v_hp(pz[:, :]), bc_hp(ea[:, 32:64]), ALU.mult),
                 PZ + [BB("ea")], [BB("tmp")])
            P.op("dve", lambda e: e.tensor_tensor(yacc[:], yacc[:], tmp[:], ALU.add), [BB("yacc"), BB("tmp")], [BB("yacc")])
            P.op("pool", lambda e: e.tensor_tensor(v_hp(tmp[:]), v_hp(T["xh"][:]), bc_hp(dsk[:]), ALU.mult),
                 [BB("L", i, "xh"), BB("dsk"), BB("tmp")], [BB("tmp")])
            P.op("dve", lambda e: e.tensor_tensor(yacc[:], yacc[:], tmp[:], ALU.add), [BB("yacc"), BB("tmp")], [BB("yacc")])
            P.op("dve", lambda e: e.tensor_tensor(yacc[:], yacc[:], T["sz"][:], ALU.mult), [BB("yacc"), BB("L", i, "sz")], [BB("yacc")])
            P.op("dve", lambda e: e.memset(ss[:], 0.0), [], [BB("ss")])
            P.op("act", lambda e: e.activation(tmp[:], yacc[:], AF.Square, accum_out=ss[:, 0:1]),
                 [BB("yacc"), BB("ss"), BB("tmp")], [BB("tmp"), BB("ss")])
            P.op("dve", lambda e: e.tensor_scalar(ss[:, 1:2], ss[:, 0:1], 1.0 / 2048, EPS, ALU.mult, ALU.add), [BB("ss")], [BB("ss")])
            P.op("act", lambda e: e.activation(ss[:, 2:3], ss[:, 1:2], AF.Sqrt), [BB("ss")], [BB("ss")])
            P.op("dve", lambda e: e.reciprocal(ss[:, 3:4], ss[:, 2:3]), [BB("ss")], [BB("ss")])
            P.op("dve", lambda e: e.scalar_tensor_tensor(tmp[:], yacc[:], ss[:, 3:4], ssdn[:], ALU.mult, ALU.mult),
                 [BB("yacc"), BB("ss"), BB("ssdn"), BB("tmp")], [BB("tmp")])
            for fc in range(16):
                P.op("pe", lambda e, fc=fc: e.transpose(pz[:, fc * 128:(fc + 1) * 128], tmp[:, fc * 128:(fc + 1) * 128], K.ident),
                     [BB("tmp"), BB("tri")], [PZ[fc // 4]])
            P.op("act", lambda e: e.copy(mixsb[:], pz[:, :].rearrange("p (f t) -> p f t", t=128)), PZ, [BB("mixsb")])
            P.dma([("sp", S["MIXT"][16:32, :, c * 128:(c + 1) * 128].rearrange("fc f t -> f fc t"), mixsb[:])],
                  [BB("mixsb")], [BB("MIXT", "s", c)], BB("mixsb"))
            update(i, 1)

        for c in range(15, -1, -1):
            full_chunk(c)


def phase_pool(K):
    nc, P, I, S, BB = K.nc, K.P, K.I, K.S, K.BB
    K.stage_late2()
    slow = {"allow_slow_non_contiguous": True}
    with contextlib.ExitStack() as ph:
        sb = lambda n, s, dt: ph.enter_context(nc.sbuf_tensor("p_" + n, s, dt))
        vch = [sb("vch%d" % i, [128, 2048], BF16) for i in range(10)]
        PM = sb("PM", [128, 4, 9, 128], BF16)
        invc = sb("invc", [128, 16, 4], F32)
        poolw = sb("poolw", [128, 4, 4, 512], BF16)
        psc = sb("psc", [128, 16], F32)
        diff32 = sb("diff32", [128, 2048], F32)
        diffT = sb("diffT", [128, 16, 128], BF16)
        mixsb = sb("mixsb", [128, 16, 128], BF16)
        pq = ph.enter_context(nc.psum_tensor("p_pq", [128, 2048], F32))
        ptr = ph.enter_context(nc.psum_tensor("p_ptr", [128, 2048], F32))
        PQ = [BB("pq", i) for i in range(4)]
        PT = [BB("ptr", i) for i in range(4)]
        P.dma([("pool", PM[:], I["pm"])], [], [BB("PM")], BB("PM"))
        P.dma([("sp", invc[:], I["invc"])], [], [BB("invc")], BB("invc"))
        for g in range(4):
            P.dma([("act", poolw[:, g], S["poolwB"][g].rearrange("(cci p) d -> p cci d", p=128))],
                  [BB("poolwB", 0)], [BB("poolw", g)], BB("poolw", g))
        P.dma([("sp", psc[:], I["pool_scale"][0, :].rearrange("(c p) -> p c", p=128), slow)], [], [BB("psc")], BB("psc"))
        loaded = set()

        def need(k):
            if k in loaded:
                return
            loaded.add(k)
            P.dma([("sp", vch[k % 10][:], S["VV"][k * 128:(k + 1) * 128, :])],
                  [], [BB("vch", k % 10)], BB("vch", k % 10))

        def chunk(o):
            for k in range(max(0, o - 4), min(19, o + 4) + 1):
                need(k)
            for w in range(4):
                nd = POOL_ND[w]
                ds = [d for d in range(-nd, nd + 1) if 0 <= o + d <= 19]
                for idx, d in enumerate(ds):
                    P.op("pe", lambda e, w=w, d=d, idx=idx, n=len(ds): e.matmul(
                        pq[:, w * 512:(w + 1) * 512], PM[:, w, d + 4, :], vch[(o + d) % 10][:, w * 512:(w + 1) * 512],
                        start=(idx == 0), stop=(idx == n - 1)), [BB("PM"), BB("vch", (o + d) % 10)], [PQ[w]])
                P.op("dve", lambda e, w=w: e.scalar_tensor_tensor(
                    diff32[:, w * 512:(w + 1) * 512], pq[:, w * 512:(w + 1) * 512], invc[:, o, w:w + 1],
                    vch[o % 10][:, w * 512:(w + 1) * 512], ALU.mult, ALU.subtract),
                    [PQ[w], BB("invc"), BB("vch", o % 10)], [BB("diff32")])
            for cc in range(16):
                P.op("pe", lambda e, cc=cc: e.transpose(ptr[:, cc * 128:(cc + 1) * 128], diff32[:, cc * 128:(cc + 1) * 128], K.ident),
                     [BB("diff32"), BB("tri")], [PT[cc // 4]])
            P.op("act", lambda e: e.copy(diffT[:], ptr[:, :].rearrange("p (c t) -> p c t", t=128)), PT, [BB("diffT")])
            for g in range(4):
                for dcc in range(4):
                    fc = 4 * g + dcc
                    for cci in range(4):
                        P.op("pe", lambda e, g=g, dcc=dcc, cci=cci, fc=fc: e.matmul(
                            pq[:, fc * 128:(fc + 1) * 128], poolw[:, g, cci, dcc * 128:(dcc + 1) * 128], diffT[:, 4 * g + cci, :],
                            start=(cci == 0), stop=(cci == 3)), [BB("poolw", g), BB("diffT")], [PQ[fc // 4]])
                    P.op("act", lambda e, fc=fc: e.activation(mixsb[:, fc, :], pq[:, fc * 128:(fc + 1) * 128], AF.Copy,
                                                              scale=psc[:, fc:fc + 1]),
                         [PQ[fc // 4], BB("psc")], [BB("mixsb")])
            P.dma([("sp", S["MIXT"][0:16, :, o * 128:(o + 1) * 128].rearrange("fc f t -> f fc t"), mixsb[:])],
                  [BB("mixsb")], [BB("MIXT", "p", o)], BB("mixsb"))

        for o in range(16):
            chunk(o)


def phase_wout(K):
    nc, P, I, S, BB = K.nc, K.P, K.I, K.S, K.BB
    with contextlib.ExitStack() as ph:
        sb = lambda n, s, dt: ph.enter_context(nc.sbuf_tensor("o_" + n, s, dt))
        mixT = sb("mixT", [128, 32, 256], BF16)
        wt = [sb("wt%d" % i, [128, 8, 2048], BF16) for i in range(2)]
        xin = [sb("xin%d" % i, [128, D], F32) for i in range(2)]
        g1b = sb("g1b", [128, D], F32)
        tmpo = [sb("tmpo%d" % i, [128, 512], F32) for i in range(2)]
        acc = [ph.enter_context(nc.psum_tensor("o_acc%d" % i, [128, 512], F32)) for i in range(8)]
        P.dma([("sp", g1b[:], S["modrow"][0:1, 2 * D:3 * D].partition_broadcast(128))], [], [BB("g1b")], BB("g1b"))

        cnt = {"w": 0, "t": 0}

        def block(tb):
            split_dma(P, mixT[:], S["MIXT"][:, :, tb * 256:(tb + 1) * 256].rearrange("fc f t -> f fc t"),
                      [], [BB("mixT")], BB("mixT"), n=2, axis=1)
            for tq in range(2):
                r = tb * 256 + tq * 128
                split_dma(P, xin[tq][:], I["xl"][r:r + 128, :], [], [BB("xin", tq)], BB("xin", tq), n=2, axis=1)
            for dmh in range(2):
                for fcg in range(4):
                    wi = cnt["w"] % 2
                    cnt["w"] += 1
                    P.dma([("sp", wt[wi][:], S["w_outB"][dmh, fcg].rearrange("p (a n) -> p a n", n=2048))],
                          [BB("w_outB", dmh, fcg)], [BB("wt", wi)], BB("wt", wi))
                    for fci in range(8):
                        fc = fcg * 8 + fci
                        for tq in range(2):
                            for dmb in range(4):
                                P.op("pe", lambda e, wi=wi, fci=fci, fc=fc, tq=tq, dmb=dmb: e.matmul(
                                    acc[tq * 4 + dmb][:, :], mixT[:, fc, tq * 128:(tq + 1) * 128],
                                    wt[wi][:, fci, dmb * 512:(dmb + 1) * 512], start=(fc == 0), stop=(fc == 31)),
                                    [BB("mixT"), BB("wt", wi)], [BB("acc", tq * 4 + dmb)])
                for tq in range(2):
                    for dmb in range(4):
                        c0 = dmh * 2048 + dmb * 512
                        ti = cnt["t"] % 2
                        cnt["t"] += 1
                        P.op("dve", lambda e, tq=tq, dmb=dmb, c0=c0, ti=ti: e.tensor_tensor(
                            tmpo[ti][:], acc[tq * 4 + dmb][:, :], g1b[:, c0:c0 + 512], ALU.mult),
                            [BB("acc", tq * 4 + dmb), BB("g1b")], [BB("tmpo", ti)])
                        P.op("pool", lambda e, tq=tq, c0=c0, ti=ti: e.tensor_tensor(
                            xin[tq][:, c0:c0 + 512], xin[tq][:, c0:c0 + 512], tmpo[ti][:], ALU.add),
                            [BB("tmpo", ti), BB("xin", tq)], [BB("xin", tq)])
            for tq in range(2):
                r = tb * 256 + tq * 128
                P.dma([("act", S["X1"][r:r + 128, :], xin[tq][:])], [BB("xin", tq)], [BB("X1", r)], BB("xin", tq))

        for tb in range(8):
            block(tb)


def phase_peer_prep(K):
    nc, P, I, S, BB = K.nc, K.P, K.I, K.S, K.BB
    with contextlib.ExitStack() as ph:
        sb = lambda n, s, dt: ph.enter_context(nc.sbuf_tensor("q_" + n, s, dt))
        xt = sb("xt", [128, D], F32)
        junk = sb("junk", [128, D], BF16)
        ss = sb("ss", [128, 4], F32)
        hT = sb("hT", [128, 32, 512], BF16)
        wt = [sb("wt%d" % i, [128, 32, 256], BF16) for i in range(2)]
        qT = sb("qT", [128, 16, 512], F32)
        kin = sb("kin", [128, 16, 128], F32)
        keysT = sb("keysT", [128, 16, 128], F32)
        s_sb = [sb("s_sb%d" % i, [128, 16, 128], F32) for i in range(2)]
        s2 = sb("s2", [128, 16, 128], F32)
        m16 = sb("m16", [128, 16, 16], F32)
        cand = sb("cand", [128, 8, 256], F32)
        cand2 = sb("cand2", [128, 8, 256], F32)
        g16 = sb("g16", [128, 8, 16], F32)
        g24 = sb("g24", [128, 8, 8], F32)
        th = sb("th", [128, 8], F32)
        th2 = sb("th2", [128, 8], F32)
        s3 = sb("s3", [128, 16, 128], F32)
        m24 = sb("m24", [128, 16, 8], F32)
        e16 = sb("e16", [128, 8, 16], F32)
        zz = sb("zz", [128, 8], F32)
        tn = [sb("tn%d" % i, [128, 16], F32) for i in range(2)]
        tp = [ph.enter_context(nc.psum_tensor("q_tp%d" % i, [128, 512], F32)) for i in range(2)]
        acc = [ph.enter_context(nc.psum_tensor("q_acc%d" % i, [128, 512], F32)) for i in range(4)]
        psc = [ph.enter_context(nc.psum_tensor("q_psc%d" % i, [128, 512], F32)) for i in range(2)]
        print("peer_prep sbuf remaining", nc.sbuf_bytes_remaining)
        cnt = {"wt": 0, "acc": 0, "tp": 0, "ps": 0, "s": 0}
        P.dma([("sp", kin[:], I["peer_keys"].rearrange("hs e k -> e hs k"))], [], [BB("kin")], BB("kin"))
        for hs in range(16):
            pi = hs // 4 % 2
            P.op("pe", lambda e, hs=hs, pi=pi: e.transpose(psc[pi][:, (hs % 4) * 128:(hs % 4 + 1) * 128], kin[:, hs, :], K.ident),
                 [BB("kin"), BB("tri")], [BB("psc", pi)])
            if hs % 4 == 3:
                P.op("dve", lambda e, hs=hs, pi=pi: e.tensor_copy(
                    keysT[:, hs - 3:hs + 1, :], psc[pi][:, :].rearrange("p (a t) -> p a t", t=128)),
                    [BB("psc", pi)], [BB("keysT")])

        def block(j):
            for tq in range(4):
                r = j * 512 + tq * 128
                split_dma(P, xt[:], S["X1"][r:r + 128, :], [], [BB("xt")], BB("xt"), n=2, axis=1)
                P.op("dve", lambda e: e.memset(ss[:], 0.0), [], [BB("ss")])
                P.op("act", lambda e: e.activation(junk[:], xt[:], AF.Square, accum_out=ss[:, 0:1]),
                     [BB("xt"), BB("ss")], [BB("junk"), BB("ss")])
                P.op("dve", lambda e: e.tensor_scalar(ss[:, 1:2], ss[:, 0:1], 1.0 / D, EPS, ALU.mult, ALU.add), [BB("ss")], [BB("ss")])
                P.op("act", lambda e: e.activation(ss[:, 2:3], ss[:, 1:2], AF.Sqrt), [BB("ss")], [BB("ss")])
                P.op("dve", lambda e: e.reciprocal(ss[:, 3:4], ss[:, 2:3]), [BB("ss")], [BB("ss")])
                P.op("dve", lambda e: e.tensor_scalar(xt[:], xt[:], ss[:, 3:4], None, ALU.mult), [BB("xt"), BB("ss")], [BB("xt")])
                for kc in range(32):
                    if kc % 4 == 0:
                        ti = cnt["tp"] % 2
                        cnt["tp"] += 1
                    P.op("pe", lambda e, ti=ti, kc=kc: e.transpose(
                        tp[ti][:, (kc % 4) * 128:(kc % 4 + 1) * 128], xt[:, kc * 128:(kc + 1) * 128], K.ident),
                        [BB("xt"), BB("tri")], [BB("tp", ti)])
                    if kc % 4 == 3:
                        for k2 in range(kc - 3, kc + 1):
                            P.op("act", lambda e, ti=ti, k2=k2, tq=tq: e.activation(
                                hT[:, k2, tq * 128:(tq + 1) * 128], tp[ti][:, (k2 % 4) * 128:(k2 % 4 + 1) * 128],
                                AF.Identity, bias=K.colv[:, 2, k2:k2 + 1], scale=K.gam[:, 1, k2:k2 + 1]),
                                [BB("tp", ti), BB("gam"), BB("colv", 2)], [BB("hT", tq)])
            hT_bufs = [BB("hT", tq) for tq in range(4)]
            P.dma([("act", S["H2T"][:, :, j * 512:(j + 1) * 512].rearrange("kc p t -> p kc t"), hT[:])],
                  hT_bufs, [BB("H2T", j)], BB("hT", "k"))
            for cbk in range(8):
                wi = cnt["wt"] % 2
                cnt["wt"] += 1
                P.dma([("sp", wt[wi][:], S["wqB"][cbk].rearrange("p (kc n) -> p kc n", n=256))],
                      [BB("wqB", cbk)], [BB("wt", wi)], BB("wt", wi))
                for sub in range(2):
                    hs = cbk * 2 + sub
                    ai = cnt["acc"] % 4
                    cnt["acc"] += 1
                    for kc in range(32):
                        P.op("pe", lambda e, ai=ai, wi=wi, kc=kc, sub=sub: e.matmul(
                            acc[ai][:, :], wt[wi][:, kc, sub * 128:(sub + 1) * 128], hT[:, kc, :],
                            start=(kc == 0), stop=(kc == 31)), [BB("wt", wi)] + hT_bufs, [BB("acc", ai)])
                    P.op("dve", lambda e, ai=ai, hs=hs: e.tensor_copy(qT[:, hs, :], acc[ai][:, :]), [BB("acc", ai)], [BB("qT")])
            for tq in range(4):
                si = cnt["s"] % 2
                cnt["s"] += 1
                for hs in range(16):
                    if hs % 4 == 0:
                        pi = cnt["ps"] % 2
                        cnt["ps"] += 1
                    P.op("pe", lambda e, pi=pi, hs=hs, tq=tq: e.matmul(
                        psc[pi][:, (hs % 4) * 128:(hs % 4 + 1) * 128], qT[:, hs, tq * 128:(tq + 1) * 128], keysT[:, hs, :],
                        start=True, stop=True), [BB("qT"), BB("keysT")], [BB("psc", pi)])
                    if hs % 4 == 3:
                        P.op("act", lambda e, pi=pi, hs=hs, si=si: e.copy(
                            s_sb[si][:, hs - 3:hs + 1, :], psc[pi][:, :].rearrange("p (a t) -> p a t", t=128)),
                            [BB("psc", pi)], [BB("s_sb", si)])
                r = j * 512 + tq * 128
                P.dma([("sp", S["SS"][r:r + 128, :], s_sb[si][:].rearrange("p a t -> p (a t)"))],
                      [BB("s_sb", si)], [BB("SS", r)], BB("s_sb", si))
                topk(si, r)

        def topk(si, r):
            sv = s_sb[si]
            for hs in range(16):
                P.op("dve", lambda e, hs=hs: e.max(m16[:, hs, 0:8], sv[:, hs, :]), [BB("s_sb", si)], [BB("m16")])
                P.op("dve", lambda e, hs=hs: e.match_replace(s2[:, hs, :], m16[:, hs, 0:8], sv[:, hs, :], NEG),
                     [BB("s_sb", si), BB("m16")], [BB("s2")])
                P.op("dve", lambda e, hs=hs: e.max(m16[:, hs, 8:16], s2[:, hs, :]), [BB("s2")], [BB("m16")])
                P.op("dve", lambda e, hs=hs: e.match_replace(s3[:, hs, :], m16[:, hs, 8:16], s2[:, hs, :], NEG),
                     [BB("s2"), BB("m16")], [BB("s3")])
                P.op("dve", lambda e, hs=hs: e.max(m24[:, hs, :], s3[:, hs, :]), [BB("s3")], [BB("m24")])
            m16v = m16[:].rearrange("p (h s) k -> p h s k", s=2)
            P.op("dve", lambda e: e.tensor_tensor(
                cand[:].rearrange("p h (a b) -> p h a b", a=16),
                m16v[:, :, 0, :].unsqueeze(3).to_broadcast([128, 8, 16, 16]),
                m16v[:, :, 1, :].unsqueeze(2).to_broadcast([128, 8, 16, 16]), ALU.add), [BB("m16")], [BB("cand")])
            for h in range(8):
                P.op("dve", lambda e, h=h: e.max(g16[:, h, 0:8], cand[:, h, :]), [BB("cand")], [BB("g16")])
                P.op("dve", lambda e, h=h: e.match_replace(cand2[:, h, :], g16[:, h, 0:8], cand[:, h, :], NEG),
                     [BB("cand"), BB("g16")], [BB("cand2")])
                P.op("dve", lambda e, h=h: e.max(g16[:, h, 8:16], cand2[:, h, :]), [BB("cand2")], [BB("g16")])
                P.op("dve", lambda e, h=h: e.match_replace(cand[:, h, :], g16[:, h, 8:16], cand2[:, h, :], NEG),
                     [BB("cand2"), BB("g16"), BB("cand")], [BB("cand")])
                P.op("dve", lambda e, h=h: e.max(g24[:, h, :], cand[:, h, :]), [BB("cand")], [BB("g24")])
            ti = si
            P.op("dve", lambda e: e.tensor_tensor(e16[:], g16[:], g16[:, :, 0:1].to_broadcast([128, 8, 16]), ALU.subtract),
                 [BB("g16")], [BB("e16")])
            P.op("act", lambda e: e.activation(e16[:], e16[:], AF.Exp), [BB("e16")], [BB("e16")])
            P.op("dve", lambda e: e.tensor_reduce(zz[:], e16[:], AX.X, ALU.add), [BB("e16")], [BB("zz")])
            P.op("act", lambda e: e.activation(zz[:], zz[:], AF.Ln), [BB("zz")], [BB("zz")])
            P.op("dve", lambda e: e.tensor_tensor(zz[:], zz[:], g16[:, :, 0], ALU.add), [BB("zz"), BB("g16")], [BB("zz")])
            P.op("dve", lambda e: e.tensor_scalar(tn[ti][:, 8:16], zz[:], -1.0, None, ALU.mult), [BB("zz"), BB("tn", ti)], [BB("tn", ti)])
            m24v = m24[:].rearrange("p (h s) k -> p h s k", s=2)
            P.op("dve", lambda e: e.tensor_tensor(th[:], m24v[:, :, 0, 0], m16v[:, :, 1, 0], ALU.add), [BB("m24"), BB("m16")], [BB("th")])
            P.op("dve", lambda e: e.tensor_tensor(th2[:], m16v[:, :, 0, 0], m24v[:, :, 1, 0], ALU.add), [BB("m24"), BB("m16")], [BB("th2")])
            P.op("dve", lambda e: e.tensor_tensor(th[:], th[:], th2[:], ALU.max), [BB("th"), BB("th2")], [BB("th")])
            P.op("dve", lambda e: e.tensor_tensor(th[:], th[:], g24[:, :, 0], ALU.max), [BB("th"), BB("g24")], [BB("th")])
            P.op("dve", lambda e: e.tensor_tensor(th[:], th[:], g16[:, :, 15], ALU.add), [BB("th"), BB("g16")], [BB("th")])
            P.op("dve", lambda e: e.scalar_tensor_tensor(th[:], th[:], 0.5, tn[ti][:, 8:16], ALU.mult, ALU.add),
                 [BB("th"), BB("tn", ti)], [BB("th")])
            P.op("act", lambda e: e.activation(tn[ti][:, 0:8], th[:], AF.Exp), [BB("th"), BB("tn", ti)], [BB("tn", ti)])
            P.dma([("act", S["TN"][r:r + 128, :], tn[ti][:])], [BB("tn", ti)], [BB("TN", r)], BB("tn", ti))

        for j in range(4):
            block(j)


def phase_ustage(K):
    nc, P, I, S, BB = K.nc, K.P, K.I, K.S, K.BB
    with contextlib.ExitStack() as ph:
        sb = lambda n, s, dt: ph.enter_context(nc.sbuf_tensor("u_" + n, s, dt))
        usrc_all = [sb("usrc%d" % i, [128, D], F32) for i in range(8)]
        utsb = [sb("utsb%d" % i, [128, 32, 512], BF16) for i in range(2)]
        pt = [ph.enter_context(nc.psum_tensor("u_pt%d" % i, [128, 512], F32)) for i in range(4)]

        def group(eg):
            ui = eg % 2
            usrc = usrc_all[4 * ui:4 * ui + 4]
            for et in range(4):
                r = eg * 512 + et * 128
                split_dma(P, usrc[et][:], I["peer_u"][r:r + 128, :], [], [BB("usrc", ui, et)], BB("usrc", ui, et), n=2, axis=1)
            for dc in range(32):
                pi = dc % 4
                for et in range(4):
                    P.op("pe", lambda e, pi=pi, et=et, dc=dc: e.transpose(
                        pt[pi][:, et * 128:(et + 1) * 128], usrc[et][:, dc * 128:(dc + 1) * 128], K.ident),
                        [BB("usrc", ui, et), BB("tri")], [BB("upt", pi)])
                eng = "act" if dc % 2 == 0 else "dve"
                if eng == "act":
                    P.op("act", lambda e, pi=pi, dc=dc: e.copy(utsb[ui][:, dc, :], pt[pi][:, :]), [BB("upt", pi)], [BB("utsb", ui)])
                else:
                    P.op("dve", lambda e, pi=pi, dc=dc: e.tensor_copy(utsb[ui][:, dc, :], pt[pi][:, :]), [BB("upt", pi)], [BB("utsb", ui)])
            P.dma([("sp", S["UT"][eg].rearrange("p (dc e) -> p dc e", e=512), utsb[ui][:])],
                  [BB("utsb", ui)], [BB("UT", eg)], BB("utsb", ui))

        for eg in range(32):
            group(eg)


def phase_peer_dense(K):
    nc, P, I, S, BB = K.nc, K.P, K.I, K.S, K.BB
    phase_ustage(K)
    P.barrier()
    with contextlib.ExitStack() as ph:
        sb = lambda n, s, dt: ph.enter_context(nc.sbuf_tensor("d_" + n, s, dt))
        h2T = sb("h2T", [128, 32, 256], BF16)
        s_t = [sb("s_t%d" % i, [128, 16, 128], F32) for i in range(2)]
        tnt = [sb("tnt%d" % i, [128, 16], F32) for i in range(2)]
        AT = sb("AT", [128, 128, 256], BF16)
        wtile = [sb("wtile%d" % i, [128, 16384], BF16) for i in range(2)]
        NB = 3
        bq = [sb("bq%d" % i, [128, 8, 128], F32) for i in range(2)]
        e1q = [sb("e1q%d" % i, [128, 4, 128], F32) for i in range(2)]
        e0q = [sb("e0q%d" % i, [128, 4, 128], F32) for i in range(2)]
        wvA = [sb("wvA%d" % i, [128, 4, 128], F32) for i in range(3)]
        wvD = [sb("wvD%d" % i, [128, 4, 128], F32) for i in range(2)]
        Gh = [sb("Gh%d" % i, [128, 4, 128], F32) for i in range(3)]
        G = [sb("G%d" % i, [128, 4, 128], F32) for i in range(2)]
        ge = [sb("ge%d" % i, [128, 512], F32) for i in range(2)]
        A32 = ge
        po = [sb("po%d" % i, [128, 512], F32) for i in range(1)]
        pb = [ph.enter_context(nc.psum_tensor("d_pb%d" % i, [128, 512], F32)) for i in range(8)]
        print("peer_dense sbuf remaining", nc.sbuf_bytes_remaining)
        cnt = {"w": 0, "pa": 0, "g": 0, "h": 0, "po": 0, "d": 0, "gh": 0}

        def block(tb):
            split_dma(P, h2T[:], S["H2T"][:, :, tb * 256:(tb + 1) * 256].rearrange("kc p t -> p kc t"),
                      [], [BB("h2T")], BB("h2T"), n=2, axis=1)
            for tq in range(2):
                r = tb * 256 + tq * 128
                P.dma([("sp", s_t[tq][:].rearrange("p a t -> p (a t)"), S["SS"][r:r + 128, :]),
                       ("act", tnt[tq][:], S["TN"][r:r + 128, :])], [], [BB("s_t", tq)], BB("s_t", tq))
                P.op("dve", lambda e, tq=tq: e.tensor_tensor(
                    bq[tq][:], s_t[tq][:].rearrange("p (h s) t -> p h s t", s=2)[:, :, 0, :],
                    tnt[tq][:, 8:16].unsqueeze(2).to_broadcast([128, 8, 128]), ALU.add), [BB("s_t", tq)], [BB("bq", tq)])
                P.op("act", lambda e, tq=tq: e.activation(
                    e1q[tq][:], s_t[tq][:].rearrange("p (h s) t -> p h s t", s=2)[:, 4:8, 1, :], AF.Exp),
                    [BB("s_t", tq)], [BB("e1q", tq)])
                P.op("act", lambda e, tq=tq: e.activation(e0q[tq][:], bq[tq][:, 4:8, :], AF.Exp), [BB("bq", tq)], [BB("e0q", tq)])
            pend = []
            for eb in range(32):
                wi = cnt["w"] % 2
                cnt["w"] += 1
                ut = wtile[wi][:].rearrange("p (dc e) -> p dc e", e=512)
                P.dma([("sp", wtile[wi][:], S["UT"][eb])], [], [BB("wtile", wi)], BB("wtile", wi))
                pis = []
                for tq in range(2):
                    pi = cnt["pa"] % 4
                    cnt["pa"] += 1
                    pis.append(pi)
                    for dc in range(32):
                        P.op("pe", lambda e, pi=pi, dc=dc, tq=tq, ut=ut: e.matmul(
                            pb[pi][:, :], h2T[:, dc, tq * 128:(tq + 1) * 128], ut[:, dc, :],
                            start=(dc == 0), stop=(dc == 31)), [BB("h2T"), BB("wtile", wi)], [BB("pb", pi)])
                while pend:
                    pend.pop(0)()
                for tq in range(2):
                    gi = tq
                    HA, HD = (0, 1, 2, 3, 4), (5, 6, 7)
                    wa = {}

                    def act_head(h, tq=tq, eb=eb, wa=wa):
                        ai = cnt["h"] % 3
                        cnt["h"] += 1
                        wa[h] = ai
                        for il in range(4):
                            P.op("act", lambda e, ai=ai, h=h, tq=tq, eb=eb, il=il: e.activation(
                                wvA[ai][:, il, :], s_t[tq][:, 2 * h + 1, :], AF.Exp,
                                bias=bq[tq][:, h, eb * 4 + il:eb * 4 + il + 1]), [BB("s_t", tq), BB("bq", tq)], [BB("wvA", ai, il)])
                    for h in HA[:3]:
                        act_head(h)
                    pending = None
                    first = True
                    for h in HD + HA:
                        if h in HD:
                            di = cnt["d"] % 2
                            cnt["d"] += 1
                            P.op("dve", lambda e, di=di, h=h, tq=tq, eb=eb: e.tensor_tensor(
                                wvD[di][:], e1q[tq][:, h - 4, :].unsqueeze(1).to_broadcast([128, 4, 128]),
                                e0q[tq][:, h - 4, eb * 4:(eb + 1) * 4].unsqueeze(2).to_broadcast([128, 4, 128]), ALU.mult),
                                [BB("e1q", tq), BB("e0q", tq)], [BB("wvD", di)])
                            wsrc, wbuf = wvD[di], [BB("wvD", di)]
                        else:
                            wsrc, wbuf = wvA[wa[h]], [BB("wvA", wa[h], il) for il in range(4)]
                        if first:
                            P.op("dve", lambda e, wsrc=wsrc, h=h, tq=tq, gi=gi: e.scalar_tensor_tensor(
                                G[gi][:], wsrc[:], tnt[tq][:, h:h + 1], wsrc[:], ALU.is_ge, ALU.mult),
                                wbuf + [BB("s_t", tq)], [BB("G", gi)])
                            first = False
                            continue
                        gh = cnt["gh"] % 3
                        cnt["gh"] += 1
                        P.op("dve", lambda e, wsrc=wsrc, h=h, tq=tq, gh=gh: e.scalar_tensor_tensor(
                            Gh[gh][:], wsrc[:], tnt[tq][:, h:h + 1], wsrc[:], ALU.is_ge, ALU.mult),
                            wbuf + [BB("s_t", tq)], [BB("Gh", gh)])
                        if pending is not None:
                            P.op("dve", lambda e, hp=pending, gi=gi: e.tensor_tensor(G[gi][:], G[gi][:], Gh[hp][:], ALU.add),
                                 [BB("G", gi), BB("Gh", pending)], [BB("G", gi)])
                        pending = gh
                        if h in HA and HA.index(h) + 3 < len(HA):
                            act_head(HA[HA.index(h) + 3])
                    P.op("dve", lambda e, hp=pending, gi=gi: e.tensor_tensor(G[gi][:], G[gi][:], Gh[hp][:], ALU.add),
                         [BB("G", gi), BB("Gh", pending)], [BB("G", gi)])
                for tq in range(2):
                    gi, pi = tq, pis[tq]
                    P.op("act", lambda e, pi=pi, gi=gi: e.activation(ge[gi][:], pb[pi][:, :], AF.Gelu), [BB("pb", pi)], [BB("ge", gi)])
                for tq in range(2):
                    gi = tq
                    P.op("dve", lambda e, gi=gi: e.tensor_tensor(A32[gi][:], ge[gi][:], G[gi][:].rearrange("p a t -> p (a t)"), ALU.mult),
                         [BB("ge", gi), BB("G", gi)], [BB("ge", gi)])

                    def fin(gi=gi, eb=eb, tq=tq):
                        ti = 4 + gi
                        for sub in range(4):
                            P.op("pe", lambda e, ti=ti, gi=gi, sub=sub: e.transpose(
                                pb[ti][:, sub * 128:(sub + 1) * 128], A32[gi][:, sub * 128:(sub + 1) * 128], K.ident),
                                [BB("ge", gi), BB("tri")], [BB("pb", ti)])
                        P.op("act", lambda e, ti=ti, eb=eb, tq=tq: e.copy(
                            AT[:, eb * 4:(eb + 1) * 4, tq * 128:(tq + 1) * 128], pb[ti][:, :].rearrange("p (a t) -> p a t", t=128)),
                            [BB("pb", ti)], [BB("AT")])
                    pend.append(fin)
            while pend:
                pend.pop(0)()
            for dmh in range(2):
                for ecg in range(16):
                    wi = cnt["w"] % 2
                    cnt["w"] += 1
                    vt = wtile[wi][:].rearrange("p (a n) -> p a n", n=2048)
                    P.dma([("sp", wtile[wi][:], S["VB"][dmh, ecg])], [BB("VB", dmh, ecg)],
                          [BB("wtile", wi)], BB("wtile", wi))
                    for eci in range(8):
                        ec = ecg * 8 + eci
                        for tq in range(2):
                            for dmb in range(4):
                                P.op("pe", lambda e, vt=vt, eci=eci, ec=ec, tq=tq, dmb=dmb: e.matmul(
                                    pb[tq * 4 + dmb][:, :], AT[:, ec, tq * 128:(tq + 1) * 128], vt[:, eci, dmb * 512:(dmb + 1) * 512],
                                    start=(ec == 0), stop=(ec == 127)), [BB("AT"), BB("wtile", wi)], [BB("pb", tq * 4 + dmb)])
                for tq in range(2):
                    for dmb in range(4):
                        oi = 0
                        r = tb * 256 + tq * 128
                        c0 = dmh * 2048 + dmb * 512
                        P.op("act" if dmb % 2 == 0 else "dve",
                             (lambda e, oi=oi, tq=tq, dmb=dmb: e.copy(po[oi][:], pb[tq * 4 + dmb][:, :])) if dmb % 2 == 0 else
                             (lambda e, oi=oi, tq=tq, dmb=dmb: e.tensor_copy(po[oi][:], pb[tq * 4 + dmb][:, :])),
                             [BB("pb", tq * 4 + dmb)], [BB("po", oi)])
                        P.dma([("sp", S["PO"][r:r + 128, c0:c0 + 512], po[oi][:])], [BB("po", oi)], [BB("PO", r, c0)], BB("po", oi))

        for tb in range(8):
            block(tb)


def phase_final(K):
    nc, P, I, S, BB = K.nc, K.P, K.I, K.S, K.BB
    with contextlib.ExitStack() as ph:
        sb = lambda n, s, dt: ph.enter_context(nc.sbuf_tensor("f_" + n, s, dt))
        x1 = [sb("x1_%d" % i, [128, D], F32) for i in range(2)]
        pot = [sb("pot%d" % i, [128, D], F32) for i in range(2)]
        g2b = sb("g2b", [128, D], F32)
        fnb = sb("fnb", [128, D], F32)
        ss = sb("ss", [128, 4], F32)
        P.dma([("sp", g2b[:], S["modrow"][0:1, 5 * D:6 * D].partition_broadcast(128))], [], [BB("g2b")], BB("g2b"))
        P.dma([("act", fnb[:], I["nrm"][2:3, :].partition_broadcast(128))], [], [BB("fnb")], BB("fnb"))

        def chunk(c):
            i = c % 2
            r = c * 128
            split_dma(P, x1[i][:], S["X1"][r:r + 128, :], [], [BB("x1", i)], BB("x1", i), n=2, axis=1)
            split_dma(P, pot[i][:], S["PO"][r:r + 128, :], [], [BB("pot", i)], BB("pot", i), n=2, axis=1)
            P.op("dve", lambda e: e.tensor_tensor(pot[i][:], pot[i][:], g2b[:], ALU.mult), [BB("pot", i), BB("g2b")], [BB("pot", i)])
            P.op("pool", lambda e: e.tensor_tensor(x1[i][:], x1[i][:], pot[i][:], ALU.add), [BB("pot", i), BB("x1", i)], [BB("x1", i)])
            P.op("dve", lambda e: e.memset(ss[:], 0.0), [], [BB("ss")])
            P.op("act", lambda e: e.activation(pot[i][:], x1[i][:], AF.Square, accum_out=ss[:, 0:1]),
                 [BB("x1", i), BB("ss"), BB("pot", i)], [BB("pot", i), BB("ss")])
            P.op("dve", lambda e: e.tensor_scalar(ss[:, 1:2], ss[:, 0:1], 1.0 / D, EPS, ALU.mult, ALU.add), [BB("ss")], [BB("ss")])
            P.op("act", lambda e: e.activation(ss[:, 2:3], ss[:, 1:2], AF.Sqrt), [BB("ss")], [BB("ss")])
            P.op("dve", lambda e: e.reciprocal(ss[:, 3:4], ss[:, 2:3]), [BB("ss")], [BB("ss")])
            P.op("dve", lambda e: e.scalar_tensor_tensor(pot[i][:], x1[i][:], ss[:, 3:4], fnb[:], ALU.mult, ALU.mult),
                 [BB("x1", i), BB("ss"), BB("fnb"), BB("pot", i)], [BB("pot", i)])
            P.dma([("sp", K.out[r:r + 128, :], pot[i][:])], [BB("pot", i)], [BB("out", r)], BB("pot", i), final=True)

        for c in range(16):
            chunk(c)


def kernel(**inputs):
    inp = {k: np.asarray(v) for k, v in inputs.items()}
    nc = build_program()
    in_maps = [prep_core(inp, b, h) for b in range(4) for h in range(2)]
    res = run_bass_kernel_spmd(nc, in_maps, core_ids=list(range(8)))
    out = np.empty((4, NT, D), np.float32)
    for ci, r in enumerate(res.results):
        b, h = divmod(ci, 2)
        o = np.asarray(r["out"])
        if h == 0:
            out[b, :OWN] = o
        else:
            out[b, OWN:] = o[::-1]
    return out
```
